# Optimizing a Trainium2 kernel written in Bass

```python
import math
import jax, jax.numpy as jnp
from jax import lax
import numpy as np

D_MODEL = 2048
BATCH = 4
SEQ = 2048
DEPTH = 1

CHUNK = 64
N_MEM = 256
RET_WIDTH = D_MODEL // 2
RET_HEADS = 8
RET_HEAD_DIM = RET_WIDTH // RET_HEADS
LRU_WIDTH = D_MODEL - RET_WIDTH
LRU_BLOCKS = 8
LRU_BLOCK_DIM = LRU_WIDTH // LRU_BLOCKS
CONV_WIDTH = 4
LRU_C = 8.0
IN_COLS = 4 * RET_WIDTH + 2 * LRU_WIDTH
XATTN_HEADS = 4
XATTN_HEAD_DIM = D_MODEL // XATTN_HEADS
PEER_HEADS = 8
PEER_N_KEYS = 128
PEER_N_EXPERTS = PEER_N_KEYS * PEER_N_KEYS
PEER_QUERY_DIM = 256
PEER_HALF = PEER_QUERY_DIM // 2
PEER_TOPK = 16
PEER_BLOCK = 128
ROPE_BASE = 10000.0
NORM_EPS = 1e-6
GN_EPS = 1e-5

kernel_name = 'hybrid_retention_rglru_peer_layer'


def rms_norm(x, g):
    xf = x.astype(jnp.float32)
    y = xf * lax.rsqrt(jnp.mean(xf * xf, axis=-1, keepdims=True) + NORM_EPS)
    return (y * g.astype(jnp.float32)).astype(x.dtype)


def rotary(t, pos):
    half = t.shape[-1] // 2
    inv_freq = ROPE_BASE ** (-jnp.arange(half, dtype=jnp.float32) / half)
    ang = pos[:, None] * inv_freq[None, :]
    cos = jnp.cos(ang)[None, :, None, :].astype(t.dtype)
    sin = jnp.sin(ang)[None, :, None, :].astype(t.dtype)
    t1, t2 = t[..., :half], t[..., half:]
    return jnp.concatenate([t1 * cos - t2 * sin, t1 * sin + t2 * cos], axis=-1)


def chunkwise_retention(q, k, v):
    b, s, h, d = q.shape
    nc = s // CHUNK
    log_gamma = jnp.log1p(-jnp.exp2(-5.0 - jnp.arange(h, dtype=jnp.float32)))
    idx = jnp.arange(CHUNK, dtype=jnp.float32)
    d_intra = jnp.exp(jnp.abs(idx[:, None] - idx[None, :]) * log_gamma[:, None, None])
    q_decay = jnp.exp((idx + 1.0)[None, :] * log_gamma[:, None])
    k_decay = jnp.exp((CHUNK - 1.0 - idx)[None, :] * log_gamma[:, None])
    chunk_decay = jnp.exp(CHUNK * log_gamma)

    def to_chunks(t):
        return t.astype(jnp.float32).reshape(b, nc, CHUNK, h, d).transpose(1, 0, 3, 2, 4)

    qc, kc, vc = to_chunks(q), to_chunks(k), to_chunks(v)
    scores = jnp.einsum('nbhid,nbhjd->nbhij', qc, kc) * d_intra[None, None]
    intra = jnp.einsum('nbhij,nbhje->nbhie', scores, vc)

    def step(state, qkv):
        q_, k_, v_ = qkv
        o = jnp.einsum('bhid,bhde->bhie', q_ * q_decay[None, :, :, None], state)
        state = chunk_decay[None, :, None, None] * state + jnp.einsum(
            'bhjd,bhje->bhde', k_ * k_decay[None, :, :, None], v_)
        return state, o

    state0 = jnp.zeros((b, h, d, d), jnp.float32)
    _, inter = lax.scan(step, state0, (qc, kc, vc))
    o = intra + inter
    return o.transpose(1, 0, 3, 2, 4).reshape(b, s, h, d)


def head_group_norm(o, g):
    mu = jnp.mean(o, axis=-1, keepdims=True)
    var = jnp.mean(jnp.square(o - mu), axis=-1, keepdims=True)
    y = (o - mu) * lax.rsqrt(var + GN_EPS)
    return y.reshape(o.shape[0], o.shape[1], -1) * g.astype(jnp.float32)


def rg_lru_group(xb, yb, conv_w, conv_b, w_rg, b_rg, w_ig, b_ig, lru_lambda, lru_norm_g):
    b, s, _ = xb.shape
    xp = jnp.pad(xb, ((0, 0), (CONV_WIDTH - 1, 0), (0, 0)))
    xc = conv_b
    for tap in range(CONV_WIDTH):
        xc = xc + xp[:, tap:tap + s] * conv_w[tap]
    xg = xc.reshape(b, s, LRU_BLOCKS, LRU_BLOCK_DIM)
    r = jax.nn.sigmoid(jnp.einsum('bsgi,gij->bsgj', xg, w_rg) + b_rg).reshape(b, s, LRU_WIDTH)
    i = jax.nn.sigmoid(jnp.einsum('bsgi,gij->bsgj', xg, w_ig) + b_ig).reshape(b, s, LRU_WIDTH)
    log_a = LRU_C * r.astype(jnp.float32) * jax.nn.log_sigmoid(lru_lambda.astype(jnp.float32))
    a = jnp.exp(log_a)
    u = jnp.sqrt(-jnp.expm1(2.0 * log_a)) * (i * xc).astype(jnp.float32)

    def combine(e1, e2):
        a1, b1 = e1
        a2, b2 = e2
        return a1 * a2, a2 * b1 + b2

    _, hseq = lax.associative_scan(combine, (a, u), axis=1)
    y = hseq * jax.nn.gelu(yb.astype(jnp.float32))
    return rms_norm(y.astype(xb.dtype), lru_norm_g)


def memory_cross_attention(xn, mn, w_xq, w_xk, w_xv, w_xo):
    b, s, _ = xn.shape
    m = mn.shape[1]
    q = (xn @ w_xq).reshape(b, s, XATTN_HEADS, XATTN_HEAD_DIM)
    k = (mn @ w_xk).reshape(b, m, XATTN_HEADS, XATTN_HEAD_DIM)
    v = (mn @ w_xv).reshape(b, m, XATTN_HEADS, XATTN_HEAD_DIM)
    scores = jnp.einsum('bshd,bmhd->bhsm', q, k).astype(jnp.float32) * (XATTN_HEAD_DIM ** -0.5)
    p = jax.nn.softmax(scores, axis=-1).astype(v.dtype)
    o = jnp.einsum('bhsm,bmhd->bshd', p, v).reshape(b, s, D_MODEL)
    return o @ w_xo


def peer_ffn(xn, w_q, sub_keys, u_tab, v_tab):
    b, s, d = xn.shape
    t = b * s
    xt = xn.reshape(t, d)
    q = (xt @ w_q).reshape(t, PEER_HEADS, 2, PEER_HALF)
    sub_scores = jnp.einsum('thcd,hckd->thck', q, sub_keys).astype(jnp.float32)
    vals, ids = lax.top_k(sub_scores, PEER_TOPK)
    cand = vals[:, :, 0, :, None] + vals[:, :, 1, None, :]
    top_sc, pos = lax.top_k(cand.reshape(t, PEER_HEADS, PEER_TOPK * PEER_TOPK), PEER_TOPK)
    i1 = jnp.take_along_axis(ids[:, :, 0], pos // PEER_TOPK, axis=-1)
    i2 = jnp.take_along_axis(ids[:, :, 1], pos % PEER_TOPK, axis=-1)
    eid = (i1 * PEER_N_KEYS + i2).reshape(t, PEER_HEADS * PEER_TOPK)
    gate = jax.nn.softmax(top_sc, axis=-1).reshape(t, PEER_HEADS * PEER_TOPK)
    nb = t // PEER_BLOCK

    def block(args):
        xb, eb, gb = args
        u = u_tab[eb]
        act = jax.nn.gelu(jnp.einsum('td,tkd->tk', xb, u).astype(jnp.float32))
        return jnp.einsum('tk,tkd->td', (gb * act).astype(xb.dtype), v_tab[eb])

    out = lax.map(block, (xt.reshape(nb, PEER_BLOCK, d),
                          eid.reshape(nb, PEER_BLOCK, -1),
                          gate.reshape(nb, PEER_BLOCK, -1)))
    return out.reshape(b, s, d)


def hybrid_layer(h, mem, mix_norm_g, w_in, ret_gn_g, conv_w, conv_b, w_rg, b_rg, w_ig, b_ig,
                 lru_lambda, lru_norm_g, w_out, xattn_norm_g, mem_norm_g, w_xq, w_xk, w_xv, w_xo,
                 ffn_norm_g, peer_w_q, peer_sub_keys, peer_u, peer_v):
    b, s, _ = h.shape
    xn = rms_norm(h, mix_norm_g)
    proj = xn @ w_in
    q, k, v, g, xb, yb = jnp.split(
        proj, [RET_WIDTH, 2 * RET_WIDTH, 3 * RET_WIDTH, 4 * RET_WIDTH, 4 * RET_WIDTH + LRU_WIDTH], axis=-1)
    pos = jnp.arange(s, dtype=jnp.float32)
    q = rotary(q.reshape(b, s, RET_HEADS, RET_HEAD_DIM), pos)
    k = rotary(k.reshape(b, s, RET_HEADS, RET_HEAD_DIM), pos) * (RET_HEAD_DIM ** -0.5)
    v = v.reshape(b, s, RET_HEADS, RET_HEAD_DIM)
    ret = head_group_norm(chunkwise_retention(q, k, v), ret_gn_g)
    ret = (jax.nn.silu(g.astype(jnp.float32)) * ret).astype(h.dtype)
    lru = rg_lru_group(xb, yb, conv_w, conv_b, w_rg, b_rg, w_ig, b_ig, lru_lambda, lru_norm_g)
    h = h + jnp.concatenate([ret, lru], axis=-1) @ w_out
    h = h + memory_cross_attention(rms_norm(h, xattn_norm_g), rms_norm(mem, mem_norm_g),
                                   w_xq, w_xk, w_xv, w_xo)
    h = h + peer_ffn(rms_norm(h, ffn_norm_g), peer_w_q, peer_sub_keys, peer_u, peer_v)
    return h


def setup_inputs(seed: int = 0) -> dict:
    key = jax.random.key(seed)
    ks = jax.random.split(key, 32)
    f32 = jnp.float32
    L = DEPTH

    def nrm(k, shape, scale):
        return scale * jax.random.normal(k, shape, f32)

    def gain(k, shape):
        return 1.0 + 0.02 * jax.random.normal(k, shape, f32)

    lru_a = jax.random.uniform(ks[10], (L, LRU_WIDTH), f32, 0.9, 0.999)
    return {
        'x': jax.random.normal(ks[0], (BATCH, SEQ, D_MODEL), f32),
        'mem': jax.random.normal(ks[1], (BATCH, N_MEM, D_MODEL), f32),
        'mix_norm_g': gain(ks[2], (L, D_MODEL)),
        'w_in': nrm(ks[3], (L, D_MODEL, IN_COLS), D_MODEL ** -0.5),
        'ret_gn_g': gain(ks[4], (L, RET_WIDTH)),
        'conv_w': nrm(ks[5], (L, CONV_WIDTH, LRU_WIDTH), CONV_WIDTH ** -0.5),
        'conv_b': nrm(ks[6], (L, LRU_WIDTH), 0.01),
        'w_rg': nrm(ks[7], (L, LRU_BLOCKS, LRU_BLOCK_DIM, LRU_BLOCK_DIM), LRU_BLOCK_DIM ** -0.5),
        'b_rg': nrm(ks[8], (L, LRU_BLOCKS, LRU_BLOCK_DIM), 0.01),
        'w_ig': nrm(ks[9], (L, LRU_BLOCKS, LRU_BLOCK_DIM, LRU_BLOCK_DIM), LRU_BLOCK_DIM ** -0.5),
        'b_ig': nrm(ks[11], (L, LRU_BLOCKS, LRU_BLOCK_DIM), 0.01),
        'lru_lambda': jnp.log(lru_a) - jnp.log1p(-lru_a),
        'lru_norm_g': gain(ks[12], (L, LRU_WIDTH)),
        'w_out': nrm(ks[13], (L, D_MODEL, D_MODEL), D_MODEL ** -0.5),
        'xattn_norm_g': gain(ks[14], (L, D_MODEL)),
        'mem_norm_g': gain(ks[15], (L, D_MODEL)),
        'w_xq': nrm(ks[16], (L, D_MODEL, D_MODEL), D_MODEL ** -0.5),
        'w_xk': nrm(ks[17], (L, D_MODEL, D_MODEL), D_MODEL ** -0.5),
        'w_xv': nrm(ks[18], (L, D_MODEL, D_MODEL), D_MODEL ** -0.5),
        'w_xo': nrm(ks[19], (L, D_MODEL, D_MODEL), D_MODEL ** -0.5),
        'ffn_norm_g': gain(ks[20], (L, D_MODEL)),
        'peer_w_q': nrm(ks[21], (L, D_MODEL, PEER_HEADS * PEER_QUERY_DIM), D_MODEL ** -0.5),
        'peer_sub_keys': nrm(ks[22], (L, PEER_HEADS, 2, PEER_N_KEYS, PEER_HALF), PEER_HALF ** -0.5),
        'peer_u': nrm(ks[23], (L, PEER_N_EXPERTS, D_MODEL), D_MODEL ** -0.5),
        'peer_v': nrm(ks[24], (L, PEER_N_EXPERTS, D_MODEL), 0.25),
        'final_norm_g': gain(ks[25], (D_MODEL,)),
    }


def reference(x, mem, mix_norm_g, w_in, ret_gn_g, conv_w, conv_b, w_rg, b_rg, w_ig, b_ig,
              lru_lambda, lru_norm_g, w_out, xattn_norm_g, mem_norm_g, w_xq, w_xk, w_xv, w_xo,
              ffn_norm_g, peer_w_q, peer_sub_keys, peer_u, peer_v, final_norm_g):
    h = x
    for l in range(DEPTH):
        h = hybrid_layer(h, mem, mix_norm_g[l], w_in[l], ret_gn_g[l], conv_w[l], conv_b[l],
                         w_rg[l], b_rg[l], w_ig[l], b_ig[l], lru_lambda[l], lru_norm_g[l],
                         w_out[l], xattn_norm_g[l], mem_norm_g[l], w_xq[l], w_xk[l], w_xv[l],
                         w_xo[l], ffn_norm_g[l], peer_w_q[l], peer_sub_keys[l], peer_u[l], peer_v[l])
    return rms_norm(h, final_norm_g)
```

```python
import os
import numpy as np
import concourse.bass as bass
import concourse.mybir as mybir
from concourse.bass import IndirectOffsetOnAxis
from concourse.bass_utils import run_bass_kernel_spmd
from contextlib import ExitStack

F32 = mybir.dt.float32
BF16 = mybir.dt.bfloat16
I32 = mybir.dt.int32
AF = mybir.ActivationFunctionType
ALU = mybir.AluOpType
AX = mybir.AxisListType

D = 2048
NORM_EPS = 1e-6
GN_EPS = 1e-5
NEG = -1.0e30


class Prog:
    ENGS = ("pe", "act", "dve", "pool", "sp")

    def __init__(self, nc):
        self.nc = nc
        self.ins = []
        self.state = {}
        self.fence_idx = None
        self.n_dma_sems = {"sp": 24, "pool": 48}

    @staticmethod
    def _norm(r):
        return r if isinstance(r, tuple) else (r, None)

    def _deps_for(self, res, is_write, out):
        name, key = res
        st = self.state.setdefault(name, {"W": {}, "R": {}})
        if key is None:
            keys = set(st["W"].keys()) | set(st["R"].keys())
        else:
            keys = (key, None)
        for k in keys:
            if k in st["W"]:
                i = st["W"][k]
                out[i] = max(out.get(i, 0), 2 if not is_write else 1)
            if is_write:
                for i in st["R"].get(k, ()):
                    out[i] = max(out.get(i, 0), 1)

    def _commit(self, res, is_write, idx):
        name, key = res
        st = self.state[name]
        if is_write:
            if key is None:
                st["W"] = {None: idx}
                st["R"] = {}
            else:
                st["W"][key] = idx
                st["R"][key] = []
        else:
            st["R"].setdefault(key, []).append(idx)

    def op(self, eng, fn, reads=(), writes=(), dma=False):
        reads = [self._norm(r) for r in reads]
        writes = [self._norm(w) for w in writes]
        idx = len(self.ins)
        deps = {}
        for r in reads:
            self._deps_for(r, False, deps)
        for w in writes:
            self._deps_for(w, True, deps)
        for r in reads:
            self._commit(r, False, idx)
        for w in writes:
            self._commit(w, True, idx)
        deps.pop(idx, None)
        if self.fence_idx is not None:
            deps[self.fence_idx] = 2
        self.ins.append(dict(eng=eng, fn=fn, deps=deps, dma=dma, signal=dma))
        return idx

    def pe(self, fn, reads=(), writes=()):
        return self.op("pe", fn, reads, writes)

    def act(self, fn, reads=(), writes=()):
        return self.op("act", fn, reads, writes)

    def dve(self, fn, reads=(), writes=()):
        return self.op("dve", fn, reads, writes)

    def pool(self, fn, reads=(), writes=()):
        return self.op("pool", fn, reads, writes)

    def dma(self, q, fn, reads=(), writes=()):
        return self.op(q, fn, reads, writes, dma=True)

    def fence(self):
        deps = {}
        for name, st in self.state.items():
            for i in st["W"].values():
                deps[i] = 2
            for l in st["R"].values():
                for i in l:
                    deps[i] = 2
        if self.fence_idx is not None:
            deps[self.fence_idx] = 2
        idx = len(self.ins)
        self.ins.append(dict(eng="pool", fn=lambda e: e.nop(), deps=deps, dma=False,
                             signal=True, fence=True))
        self.state = {}
        self.fence_idx = idx

    def emit(self):
        nc = self.nc
        ins = self.ins
        for I in ins:
            real = []
            for d, kind in I["deps"].items():
                J = ins[d]
                same = (J["eng"] == I["eng"]) and not J["dma"] and not I["dma"]
                if same and not J.get("fence") and not I.get("fence"):
                    if I["eng"] == "pe":
                        continue
                real.append(d)
            best = {}
            keep = []
            for d in real:
                J = ins[d]
                if J["dma"]:
                    keep.append(d)
                else:
                    if best.get(J["eng"], -1) < d:
                        best[J["eng"]] = d
            real = keep + list(best.values())
            I["rdeps"] = real
            for d in real:
                ins[d]["signal"] = True
        with ExitStack() as es:
            eng_sem = {e: es.enter_context(nc.semaphore("s_" + e)) for e in self.ENGS}
            dma_sems = {q: [es.enter_context(nc.semaphore("d_%s%d" % (q, i)))
                            for i in range(n)] for q, n in self.n_dma_sems.items()}
            eng_cnt = {e: 0 for e in self.ENGS}
            dma_cnt = {q: 0 for q in self.n_dma_sems}
            for I in ins:
                if I["dma"]:
                    q = I["eng"]
                    k = dma_cnt[q]
                    dma_cnt[q] += 1
                    n = len(dma_sems[q])
                    I["sem"] = dma_sems[q][k % n]
                    I["val"] = 16 * (k // n + 1)
                    I["semkey"] = (q, k % n)
                elif I["signal"]:
                    e = I["eng"]
                    eng_cnt[e] += 1
                    I["sem"] = eng_sem[e]
                    I["val"] = eng_cnt[e]
                    I["semkey"] = e
            per_eng = {e: [] for e in self.ENGS}
            for I in ins:
                per_eng[I["eng"]].append(I)
            final_dma = {}
            for I in ins:
                if I["dma"]:
                    final_dma[I["semkey"]] = (I["sem"], I["val"])
            nwaits = {e: 0 for e in self.ENGS}
            block = es.enter_context(nc.Block())

            def run(engname, eng):
                waited = {}
                for I in per_eng[engname]:
                    need = {}
                    for d in I["rdeps"]:
                        J = ins[d]
                        sk = J["semkey"]
                        if need.get(sk, (None, 0))[1] < J["val"]:
                            need[sk] = (J["sem"], J["val"])
                    if I["dma"] and I["val"] > 16:
                        sk = I["semkey"]
                        if need.get(sk, (None, 0))[1] < I["val"] - 16:
                            need[sk] = (I["sem"], I["val"] - 16)
                    for sk, (sem, val) in need.items():
                        if waited.get(sk, 0) >= val:
                            continue
                        eng.wait_ge(sem, val)
                        nwaits[engname] += 1
                        waited[sk] = val
                    r = I["fn"](eng)
                    if I["dma"]:
                        r.then_inc(I["sem"], 16)
                    elif I["signal"]:
                        r.then_inc(I["sem"], 1)
                if engname == "sp":
                    for sk, (sem, val) in final_dma.items():
                        if waited.get(sk, 0) < val:
                            eng.wait_ge(sem, val)

            @block.tensor
            def _(e):
                run("pe", e)

            @block.scalar
            def _(e):
                run("act", e)

            @block.vector
            def _(e):
                run("dve", e)

            @block.gpsimd
            def _(e):
                run("pool", e)

            @block.sync
            def _(e):
                run("sp", e)
        return {"n_ins": len(ins), "eng_cnt": eng_cnt, "dma_cnt": dma_cnt, "nwaits": nwaits}


def I_act(out, in_, func, **kw):
    return lambda e: e.activation(out=out, in_=in_, func=func, **kw)


def I_tt(out, a, b, op):
    return lambda e: e.tensor_tensor(out=out, in0=a, in1=b, op=op)


def I_ts(out, a, s1, op0, s2=None, op1=None):
    if op1 is None:
        return lambda e: e.tensor_scalar(out=out, in0=a, scalar1=s1, scalar2=None, op0=op0)
    return lambda e: e.tensor_scalar(out=out, in0=a, scalar1=s1, scalar2=s2, op0=op0, op1=op1)


def I_tss(out, a, s, op):
    return lambda e: e.tensor_single_scalar(out=out, in_=a, scalar=s, op=op)


def I_stt(out, a, s, b, op0, op1, **kw):
    return lambda e: e.scalar_tensor_tensor(out=out, in0=a, scalar=s, in1=b, op0=op0, op1=op1, **kw)


def I_mm(out, lhsT, rhs, start, stop):
    return lambda e: e.matmul(out, lhsT=lhsT, rhs=rhs, start=start, stop=stop)


def I_tr(out, in_, ident):
    return lambda e: e.transpose(out=out, in_=in_, identity=ident)


def I_acopy(out, in_):
    return lambda e: e.copy(out=out, in_=in_)


def I_copy(out, in_):
    return lambda e: e.tensor_copy(out=out, in_=in_)


def I_dma(out, in_):
    return lambda e: e.dma_start(out=out, in_=in_)


def I_recip(out, in_):
    return lambda e: e.reciprocal(out=out, in_=in_)


def I_memset(ap, v):
    return lambda e: e.memset(ap, v)


class Arena:
    def __init__(self, nc, words):
        self.t = nc.alloc_sbuf_tensor("arena", [128, words], F32)
        self.words = words
        self.top = 0
        self.peak = 0

    def alloc(self, shape, dt=F32):
        shape = list(shape)
        n = int(np.prod(shape))
        esz = 2 if dt == BF16 else 4
        words = (n * esz + 3) // 4
        words = (words + 7) // 8 * 8
        off = self.top
        self.top += words
        assert self.top <= self.words, ("SBUF arena overflow", self.top, self.words)
        self.peak = max(self.peak, self.top)
        v = self.t[:, off:off + words]
        if dt != F32:
            v = v.bitcast(dt)
        v = v[:, 0:n]
        if len(shape) == 2:
            v = v.rearrange("p (a b) -> p a b", a=shape[0])
        elif len(shape) == 3:
            v = v.rearrange("p (a b c) -> p a b c", a=shape[0], b=shape[1])
        return v


def build_program(dbg=None, n_sub=4, stop_after=None):
    nc = bass.Bass("TRN2", target_bir_lowering=False)
    P = Prog(nc)
    dbg = dbg or {}
    dbg_out = {}

    def din(name, shape, dt=F32):
        return nc.dram_tensor(name, list(shape), dt, kind="ExternalInput").ap()

    xw = din("xw", [2048, D])
    memd = din("mem", [256, D])
    flagd = din("flag", [128, 1])
    csd = din("cs", [128, 16, 128])
    maskd = din("maskT", [128, 8, 128])
    qkdd = din("qkd", [128, 16])
    g_mix = din("g_mix", [128, D])
    g_xat = din("g_xattn", [128, D])
    g_mem = din("g_mem", [128, D])
    g_ffn = din("g_ffn", [128, D])
    g_fin = din("g_final", [128, D])
    gretd = din("g_ret", [128, 1024])
    lrupd = din("lrup", [128, 8, 9])
    wgd = din("w_gates", [128, 2048])
    w_in = din("w_in", [128, 16, 6144])
    w_out = din("w_out", [128, 16, D])
    w_xq = din("w_xq", [128, 16, D])
    w_xk = din("w_xk", [128, 16, D])
    w_xv = din("w_xv", [128, 16, D])
    w_xo = din("w_xo", [128, 16, D])
    w_pq = din("peer_w_q", [128, 16, D])
    skTd = din("skT", [128, 2048])
    pu = din("peer_u", [16384, D])
    pv = din("peer_v", [16384, D])
    outd = nc.dram_tensor("out", [1024, D], F32, kind="ExternalOutput").ap()
    uv = nc.dram_tensor("uv_bf16", [16384, 2 * D], BF16, kind="Internal").ap()

    def dump(name, ap, res, shape, dt=F32):
        if name not in dbg:
            return
        o = nc.dram_tensor("dbg_" + name, list(shape), dt, kind="ExternalOutput").ap()
        dbg_out[name] = o
        P.dma("sp", I_dma(o, ap), reads=[res])

    psf = [nc.alloc_psum_tensor("psf%d" % i, [128, 512], F32) for i in range(6)]
    psb = [nc.alloc_psum_tensor("psb%d" % i, [128, 1024], BF16) for i in range(2)]
    mm_ctr = [0]

    def mmbank():
        i = mm_ctr[0] % 3
        mm_ctr[0] += 1
        return psf[i], "psf%d" % i

    tb_ctr = [0]

    def tbank():
        i = tb_ctr[0] % 2
        tb_ctr[0] += 1
        return psb[i], "psb%d" % i

    SA, SB_, SC = psf[3], psf[4], psf[5]
    slot_ctr = {"psf3": 0, "psf4": 0, "psf5": 0}

    def slot(bank, name, w=128):
        ns = 512 // w
        s = slot_ctr[name] % ns
        slot_ctr[name] += 1
        return bank[:, s * w:(s + 1) * w], (name, "%d_%d" % (w, s)) if False else (name, None)


    A = Arena(nc, 53200)
    ident = A.alloc([128], BF16)
    ones_bf = A.alloc([8], BF16)
    one_f = A.alloc([8])
    qkd = A.alloc([16])
    flag = A.alloc([8])
    statef = A.alloc([8, 128])
    stateb = [A.alloc([8, 128], BF16), A.alloc([8, 128], BF16)]
    lstate = A.alloc([8])
    xhist = A.alloc([8, 3])
    eid = A.alloc([4, 128], I32)
    gate = A.alloc([4, 128])
    rstd_lru = A.alloc([8])
    stat = A.alloc([32, 16])
    h = A.alloc([4, D])
    xn = A.alloc([4, D], BF16)
    junk_bufs = [A.alloc([D], BF16), A.alloc([D], BF16)]
    junk_ctr = [0]

    def junkA():
        i = junk_ctr[0] % 2
        junk_ctr[0] += 1
        return junk_bufs[i], "junkA%d" % i
    cb = [None] * 4
    kTm = A.alloc([16, 256], BF16)
    vm = A.alloc([2, D], BF16)
    mark_always = A.top
    featT = A.alloc([16, 512], BF16)
    mark_feat = A.top
    actT = A.alloc([16, 512], BF16)
    wbuf = [A.alloc([16, 512], BF16), A.alloc([16, 512], BF16)]
    gbuf = A.alloc([D])
    mark_dense = A.top

    stat_ctr = [0]

    def newstat():
        i = stat_ctr[0] % 32
        stat_ctr[0] += 1
        return stat[:, i, :], ("stat", i)

    P.pool(I_memset(ident, 1.0), writes=["ident"])
    P.pool(lambda e: e.affine_select(out=ident, in_=ident, pattern=[[-1, 128]],
                                     compare_op=ALU.is_equal, fill=0.0, base=0,
                                     channel_multiplier=1), reads=["ident"], writes=["ident"])
    P.pool(I_memset(ones_bf, 1.0), writes=["ones_bf"])
    P.pool(I_memset(one_f, 1.0), writes=["one_f"])
    P.pool(I_memset(statef, 0.0), writes=["statef"])
    P.pool(I_memset(stateb[0], 0.0), writes=["stateb0"])
    P.pool(I_memset(stateb[1], 0.0), writes=["stateb1"])
    P.pool(I_memset(lstate, 0.0), writes=["lstate"])
    P.pool(I_memset(xhist, 0.0), writes=["xhist"])
    P.dma("sp", I_dma(qkd, qkdd), writes=["qkd"])
    P.dma("sp", I_dma(flag[:, 0:1], flagd), writes=["flag"])

    sb_par = [0] * 8
    w_ctr = [0]

    conv = {"next": 0, "quota": 4, "rate": 0.65}

    def conv_issue(k):
        if k < 256:
            conv["acc"] = conv.get("acc", 0.0) + k * conv["rate"]
            k = int(conv["acc"])
            conv["acc"] -= k
        while k > 0 and conv["next"] < 256:
            t = conv["next"]
            conv["next"] += 1
            k -= 1
            tile_i, half = t // 2, t % 2
            src = pu if half == 0 else pv
            r0 = tile_i * 128
            if os.environ.get("KD2D", "1") == "1":
                P.dma("pool", I_dma(uv[r0:r0 + 128, half * D:(half + 1) * D], src[r0:r0 + 128, :]),
                      writes=[("uv", t)])
                continue
            j = t % 4
            P.dma("pool", I_dma(cb[j], src[r0:r0 + 128, :]), writes=["cb%d" % j])
            P.dma("sp", I_dma(uv[r0:r0 + 128, half * D:(half + 1) * D], cb[j]), reads=["cb%d" % j],
                  writes=[("uv", t)])

    def load_w(wd, col0):
        i = w_ctr[0] % 2
        w_ctr[0] += 1
        res = "wbuf%d" % i
        P.dma("pool", I_dma(wbuf[i], wd[:, :, col0:col0 + 512]), writes=[res])
        conv_issue(conv["quota"])
        return wbuf[i], res

    pend = {}

    def stream(groups, body, nxt=None):
        bufs = [None] * len(groups)
        key0 = (id(groups[0][0]), groups[0][1])
        if pend.get("key") == key0:
            bufs[0] = pend["buf"]
        else:
            bufs[0] = load_w(*groups[0][:2])
        pend.clear()
        for i in range(len(groups)):
            if i + 1 < len(groups):
                bufs[i + 1] = load_w(*groups[i + 1][:2])
            elif nxt is not None and os.environ.get("KPREF", "1") == "1":
                pend["key"] = (id(nxt[0]), nxt[1])
                pend["buf"] = load_w(*nxt)
            body(i, bufs[i][0], bufs[i][1])

    def load_gain(gd):
        P.dma("sp", I_dma(gbuf, gd), writes=["gbuf"])

    def norm_block(src, src_res, n, dstT=None, dstT_res=None, col0=0, keep_rstd=None):
        s, sres = newstat()
        jk, jkres = junkA()
        P.act(I_act(jk, src, AF.Square, accum_out=s[:, 0:1]), reads=[src_res], writes=[sres, jkres])
        P.act(I_act(s[:, 1:2], s[:, 0:1], AF.Sqrt, scale=1.0 / D, bias=NORM_EPS),
              reads=[sres], writes=[sres])
        P.dve(I_recip(s[:, 2:3], s[:, 1:2]), reads=[sres], writes=[sres])
        P.dve(I_stt(xn[:, n, :], src, s[:, 2:3], gbuf, ALU.mult, ALU.mult),
              reads=[src_res, sres, "gbuf"], writes=[("xn", n)])
        if dstT is None:
            return s, sres
        for half in range(2):
            tb, tbres = tbank()
            for j in range(8):
                kc = half * 8 + j
                P.pe(I_tr(tb[:, j * 128:(j + 1) * 128], xn[:, n, kc * 128:(kc + 1) * 128], ident),
                     reads=[("xn", n), "ident"], writes=[tbres])
            P.act(I_acopy(dstT[:, half * 8:(half + 1) * 8, col0:col0 + 128],
                          tb[:, :].rearrange("p (a b) -> p a b", a=8)),
                  reads=[tbres], writes=[(dstT_res, col0 // 128)])
        return s, sres

    def norm_multi(items, dstT, dstT_res):
        st_ = []
        for (src, src_res, n, col0) in items:
            s, sres = newstat()
            jk, jkres = junkA()
            P.act(I_act(jk, src, AF.Square, accum_out=s[:, 0:1]), reads=[src_res], writes=[sres, jkres])
            st_.append((s, sres))
        for (s, sres) in st_:
            P.act(I_act(s[:, 1:2], s[:, 0:1], AF.Sqrt, scale=1.0 / D, bias=NORM_EPS),
                  reads=[sres], writes=[sres])
        for (s, sres) in st_:
            P.dve(I_recip(s[:, 2:3], s[:, 1:2]), reads=[sres], writes=[sres])
        for (src, src_res, n, col0), (s, sres) in zip(items, st_):
            P.dve(I_stt(xn[:, n, :], src, s[:, 2:3], gbuf, ALU.mult, ALU.mult),
                  reads=[src_res, sres, "gbuf"], writes=[("xn", n)])
        for (src, src_res, n, col0) in items:
            for half in range(2):
                tb, tbres = tbank()
                for j in range(8):
                    kc = half * 8 + j
                    P.pe(I_tr(tb[:, j * 128:(j + 1) * 128], xn[:, n, kc * 128:(kc + 1) * 128], ident),
                         reads=[("xn", n), "ident"], writes=[tbres])
                P.act(I_acopy(dstT[:, half * 8:(half + 1) * 8, col0:col0 + 128],
                              tb[:, :].rearrange("p (a b) -> p a b", a=8)),
                      reads=[tbres], writes=[(dstT_res, col0 // 128)])

    def phase_norm1(st):
        P.fence()
        A.top = mark_dense
        xs = [A.alloc([D]), A.alloc([D])]
        load_gain(g_mix)
        for pr in range(2):
            items = []
            for n in (2 * pr, 2 * pr + 1):
                r0 = st * 512 + n * 128
                P.dma("sp", I_dma(xs[n % 2], xw[r0:r0 + 128, :]), writes=["xs%d" % (n % 2)])
                conv_issue(3)
                items.append((xs[n % 2], "xs%d" % (n % 2), n, n * 128))
            norm_multi(items, actT, "actT")

    def phase_ret(st, full):
        P.fence()
        A.top = mark_dense
        cs_l = A.alloc([4, 128])
        P.dma("sp", I_dma(cs_l, csd[:, st * 4:(st + 1) * 4, :]), writes=["cs_l"])
        if full:
            maskT = A.alloc([8, 128])
            gret = A.alloc([1024])
            P.dma("sp", I_dma(maskT, maskd), writes=["maskT"])
            P.dma("sp", I_dma(gret, gretd), writes=["gret"])
        qrot = A.alloc([4, 512], BF16)
        krot = A.alloc([4, 512], BF16)
        vv = A.alloc([4, 512], BF16)
        sg = A.alloc([4, 512], BF16)
        qraw1 = A.alloc([512])
        qraw = [qraw1, qraw1]
        ta = A.alloc([4, 64])
        tb_ = A.alloc([4, 64])
        tc = ta
        td = tb_
        kt = [A.alloc([4, 128], BF16), A.alloc([4, 128], BF16)]
        qt = [A.alloc([4, 128], BF16), A.alloc([4, 128], BF16)]
        qkT = [A.alloc([6, 128], BF16) for _ in range(4)]
        pT = [A.alloc([128], BF16) for _ in range(8)]
        o_sb1 = A.alloc([4, 128])
        o_sb = [o_sb1, o_sb1]
        yn = A.alloc([4, 128])
        retb = [A.alloc([512], BF16), A.alloc([512], BF16)]
        raw_ctr = [0]
        it = [0]
        for hh in range(2):
            kinds = ["q", "k", "v", "g"] if full else ["k", "v"]
            base = {"q": 0, "k": 1024, "v": 2048, "g": 3072}
            groups = [(w_in, base[kd] + hh * 512, kd) for kd in kinds]

            def body(gi, wb, wbres, groups=groups):
                kind = groups[gi][2]
                for n in range(4):
                    conv_issue(1)
                    bank, bres = mmbank()
                    for kc in range(16):
                        P.pe(I_mm(bank[:, :], actT[:, kc, n * 128:(n + 1) * 128], wb[:, kc, :],
                                  kc == 0, kc == 15), reads=[("actT", n), wbres], writes=[bres])
                    if kind in ("q", "k"):
                        dst, dres = (qrot, "qrot") if kind == "q" else (krot, "krot")
                        ri = raw_ctr[0] % 2
                        raw_ctr[0] += 1
                        raw, rres = qraw[ri], "qraw0"
                        P.act(I_acopy(raw, bank[:, :]), reads=[bres], writes=[rres])
                        r4 = raw.rearrange("p (h t d) -> p h t d", h=4, t=2)
                        r1, r2 = r4[:, :, 0, :], r4[:, :, 1, :]
                        cosb = cs_l[:, n, 0:64].unsqueeze(1).to_broadcast([128, 4, 64])
                        sinb = cs_l[:, n, 64:128].unsqueeze(1).to_broadcast([128, 4, 64])
                        d4 = dst[:, n, :].rearrange("p (h t d) -> p h t d", h=4, t=2)
                        P.dve(I_tt(ta, r1, cosb, ALU.mult), reads=[rres, "cs_l"], writes=["ta"])
                        P.dve(I_tt(tb_, r2, sinb, ALU.mult), reads=[rres, "cs_l"], writes=["tb"])
                        P.dve(I_tt(d4[:, :, 0, :], ta, tb_, ALU.subtract), reads=["ta", "tb"],
                              writes=[(dres, n)])
                        P.dve(I_tt(tc, r1, sinb, ALU.mult), reads=[rres, "cs_l"], writes=["ta"])
                        P.dve(I_tt(td, r2, cosb, ALU.mult), reads=[rres, "cs_l"], writes=["tb"])
                        P.dve(I_tt(d4[:, :, 1, :], tc, td, ALU.add), reads=["ta", "tb"],
                              writes=[(dres, n)])
                    elif kind == "v":
                        P.act(I_acopy(vv[:, n, :], bank[:, :]), reads=[bres], writes=[("vv", n)])
                    else:
                        P.act(I_act(sg[:, n, :], bank[:, :], AF.Silu), reads=[bres], writes=[("sg", n)])

            if hh == 0:
                nxt = (w_in, (0 if full else 1024) + 512)
            else:
                nxt = (w_in, 4096)
            stream(groups, body, nxt)

            H4 = range(4)
            hsl = [slice(h4 * 128, (h4 + 1) * 128) for h4 in H4]
            ctx = {}

            def front(n, hh=hh):
                c = {}
                ctx[n] = c
                conv_issue(2)
                c["ktb"], c["ktres"] = kt[n % 2], "kt%d" % (n % 2)
                P.dve(I_tt(c["ktb"], krot[:, n, :].rearrange("p (h d) -> p h d", h=4),
                           qkd[:, 8 + hh * 4:12 + hh * 4].unsqueeze(2).to_broadcast([128, 4, 128]),
                           ALU.mult), reads=[("krot", n), "qkd"], writes=[c["ktres"]])
                if not full:
                    return
                qtb, qtres = qt[n % 2], "qt%d" % (n % 2)
                P.dve(I_tt(qtb, qrot[:, n, :].rearrange("p (h d) -> p h d", h=4),
                           qkd[:, hh * 4:hh * 4 + 4].unsqueeze(2).to_broadcast([128, 4, 128]),
                           ALU.mult), reads=[("qrot", n), "qkd"], writes=[qtres])
                qks = []
                for pr in range(2):
                    tb, tbres = tbank()
                    for jj, h4 in enumerate((2 * pr, 2 * pr + 1)):
                        o3 = jj * 384
                        P.pe(I_tr(tb[:, o3:o3 + 128], qrot[:, n, hsl[h4]], ident),
                             reads=[("qrot", n), "ident"], writes=[tbres])
                        P.pe(I_tr(tb[:, o3 + 128:o3 + 256], qtb[:, h4, :], ident),
                             reads=[qtres, "ident"], writes=[tbres])
                        P.pe(I_tr(tb[:, o3 + 256:o3 + 384], krot[:, n, hsl[h4]], ident),
                             reads=[("krot", n), "ident"], writes=[tbres])
                    qi = (n % 2) * 2 + pr
                    qk2, qk2res = qkT[qi], "qkT%d" % qi
                    P.act(I_acopy(qk2, tb[:, 0:768].rearrange("p (a b) -> p a b", a=6)),
                          reads=[tbres], writes=[qk2res])
                    qks.append((qk2, qk2res))
                c["qks"] = qks
                for h4 in H4:
                    qT_, qtT_, kT_, qkres = qk_of(c, h4)
                    P.pe(I_mm(SA[:, hsl[h4]], kT_, qT_, True, True), reads=[qkres], writes=["psf3"])
                for h4 in H4:
                    hd = hh * 4 + h4
                    pi = (n % 2) * 4 + h4
                    P.dve(I_tt(pT[pi], SA[:, hsl[h4]], maskT[:, hd, :], ALU.mult),
                          reads=["psf3", "maskT"], writes=["pT%d" % (n % 2)])

            def qk_of(c, h4):
                qk2, qk2res = c["qks"][h4 // 2]
                b3 = (h4 % 2) * 3
                return qk2[:, b3 + 0, :], qk2[:, b3 + 1, :], qk2[:, b3 + 2, :], qk2res

            def mid(n, hh=hh):
                c = ctx[n]
                curs = [sb_par[hh * 4 + h4] for h4 in H4]
                if full:
                    c["ost"], c["ostres"] = newstat()
                    c["osb"], c["osres"] = o_sb[0], "o_sb0"
                    ost, ostres, osb, osres = c["ost"], c["ostres"], c["osb"], c["osres"]
                    for h4 in H4:
                        hd = hh * 4 + h4
                        qT_, qtT_, kT_, qkres = qk_of(c, h4)
                        pi = (n % 2) * 4 + h4
                        P.pe(I_mm(SB_[:, hsl[h4]], pT[pi], vv[:, n, hsl[h4]], True, False),
                             reads=["pT%d" % (n % 2), ("vv", n)], writes=["psf4"])
                        P.pe(I_mm(SB_[:, hsl[h4]], qtT_, stateb[curs[h4]][:, hd, :], False, True),
                             reads=[qkres, ("stateb%d" % curs[h4], hd)], writes=["psf4"])
                    for h4 in H4:
                        P.act(I_act(osb[:, h4, :], SB_[:, hsl[h4]], AF.Copy, accum_out=ost[:, h4:h4 + 1]),
                              reads=["psf4"], writes=[osres, ostres])
                        jk, jkres = junkA()
                        P.act(I_act(jk[:, 0:128], SB_[:, hsl[h4]], AF.Square,
                                    accum_out=ost[:, 4 + h4:5 + h4]), reads=["psf4"],
                              writes=[ostres, jkres])
                for h4 in H4:
                    P.pe(I_mm(SC[:, hsl[h4]], c["ktb"][:, h4, :], vv[:, n, hsl[h4]], True, True),
                         reads=[c["ktres"], ("vv", n)], writes=["psf5"])
                for h4 in H4:
                    hd = hh * 4 + h4
                    P.dve(I_stt(statef[:, hd, :], statef[:, hd, :], float(BD[hd]), SC[:, hsl[h4]],
                                ALU.mult, ALU.add), reads=[("statef", hd), "psf5"],
                          writes=[("statef", hd)])
                for h4 in H4:
                    hd = hh * 4 + h4
                    cur = curs[h4]
                    P.act(I_acopy(stateb[1 - cur][:, hd, :], statef[:, hd, :]),
                          reads=[("statef", hd)], writes=[("stateb%d" % (1 - cur), hd)])
                    sb_par[hd] = 1 - cur

            def tail(n, hh=hh):
                if not full:
                    return
                c = ctx[n]
                ost, ostres, osb, osres = c["ost"], c["ostres"], c["osb"], c["osres"]
                s2, s2res = newstat()
                P.dve(I_ts(ost[:, 8:12], ost[:, 0:4], 1.0 / 128, ALU.mult), reads=[ostres],
                      writes=[ostres])
                P.dve(I_tt(ost[:, 12:16], ost[:, 8:12], ost[:, 8:12], ALU.mult), reads=[ostres],
                      writes=[ostres])
                P.dve(I_stt(s2[:, 0:4], ost[:, 4:8], 1.0 / 128, ost[:, 12:16], ALU.mult,
                            ALU.subtract), reads=[ostres], writes=[s2res])
                P.act(I_act(s2[:, 4:8], s2[:, 0:4], AF.Sqrt, bias=GN_EPS), reads=[s2res],
                      writes=[s2res])
                P.dve(I_recip(s2[:, 8:12], s2[:, 4:8]), reads=[s2res], writes=[s2res])
                for h4 in range(4):
                    P.dve(I_ts(yn[:, h4, :], osb[:, h4, :], ost[:, 8 + h4:9 + h4], ALU.subtract,
                               s2[:, 8 + h4:9 + h4], ALU.mult), reads=[osres, ostres, s2res],
                          writes=["yn"])
                ynf = yn.rearrange("p a b -> p (a b)")
                P.dve(I_tt(ynf, ynf, gret[:, hh * 512:(hh + 1) * 512], ALU.mult),
                      reads=["yn", "gret"], writes=["yn"])
                rb, rbres = retb[n % 2], "retb%d" % (n % 2)
                P.dve(I_tt(rb, ynf, sg[:, n, :], ALU.mult), reads=["yn", ("sg", n)], writes=[rbres])
                tb, tbres = tbank()
                for h4 in range(4):
                    P.pe(I_tr(tb[:, h4 * 128:(h4 + 1) * 128], rb[:, h4 * 128:(h4 + 1) * 128], ident),
                         reads=[rbres, "ident"], writes=[tbres])
                P.act(I_acopy(featT[:, hh * 4:hh * 4 + 4, n * 128:(n + 1) * 128],
                              tb[:, 0:512].rearrange("p (a b) -> p a b", a=4)),
                      reads=[tbres], writes=[("featT", n)])

            front(0)
            for n in range(4):
                mid(n)
                if n + 1 < 4:
                    front(n + 1)
                tail(n)
        if full and st == 2:
            dump("featT_ret", featT[:, 0:8, :], "featT", [128, 8, 512], BF16)
            dump("statef", statef, "statef", [128, 8, 128])

    def phase_lru(st, full):
        P.fence()
        A.top = mark_dense
        lrup = A.alloc([8, 9])
        wg = A.alloc([2048], BF16)
        P.dma("sp", I_dma(lrup, lrupd), writes=["lrup"])
        P.dma("pool", I_dma(wg, wgd), writes=["wg"])
        wg4 = wg.rearrange("p (t g j) -> p t g j", t=2, g=8)
        cl = A.alloc([8])
        cl2 = A.alloc([8])
        e1 = A.alloc([8])
        l1 = A.alloc([8])
        P.act(I_act(e1, lrup[:, :, 7], AF.Exp, scale=-1.0), reads=["lrup"], writes=["e1"])
        P.act(I_act(l1, e1, AF.Ln, bias=1.0), reads=["e1"], writes=["l1"])
        P.dve(I_ts(cl, l1, -8.0, ALU.mult), reads=["l1"], writes=["cl"])
        P.dve(I_ts(cl2, l1, -16.0, ALU.mult), reads=["l1"], writes=["cl"])
        xbt = [A.alloc([520]), A.alloc([520])]
        xc = [A.alloc([512]), A.alloc([512])]
        xcb = [A.alloc([512], BF16), A.alloc([512], BF16)]
        r_ = [A.alloc([512]), A.alloc([512])]
        i_ = [A.alloc([512]), A.alloc([512])]
        a_ = [A.alloc([512]), A.alloc([512])]
        s_ = [A.alloc([512]), A.alloc([512])]
        hseq = [A.alloc([512]) for _ in range(4)]
        gy = [A.alloc([512]), A.alloc([512])]
        y_ = [A.alloc([512]), A.alloc([512])]
        ysq = [A.alloc([512], BF16), A.alloc([512], BF16)]
        ssrow = y_[0]
        order = [("xb", 0), ("yb", 0), ("xb", 1), ("yb", 1)] if full else [("xb", 0), ("xb", 1)]
        groups = [(w_in, 4096 + (0 if k == "xb" else 1024) + cg * 512, k, cg) for k, cg in order]
        lb_ctr = [0]

        def lbank():
            i = lb_ctr[0] % 5
            lb_ctr[0] += 1
            return psf[i], "psf%d" % i

        def body(gi, wb, wbres):
            k, cg = groups[gi][2], groups[gi][3]
            for pair in range(2):
                cs4 = (pair * 2, pair * 2 + 1)
                banks = {}
                for c4 in cs4:
                    conv_issue(2)
                    bank, bres = lbank()
                    banks[c4] = (bank, bres)
                    for kc in range(16):
                        P.pe(I_mm(bank[:, :], wb[:, kc, c4 * 128:(c4 + 1) * 128], actT[:, kc, :],
                                  kc == 0, kc == 15), reads=["actT", wbres], writes=[bres])
                if k == "xb":
                    for c4 in cs4:
                        c, j = cg * 4 + c4, c4 % 2
                        bank, bres = banks[c4]
                        xb, xbres = xbt[j], "xbt%d" % j
                        P.dve(I_copy(xb[:, 0:3], xhist[:, c, :]), reads=[("xhist", c)], writes=[xbres])
                        P.act(I_acopy(xb[:, 3:515], bank[:, :]), reads=[bres], writes=[xbres])
                        P.dve(I_copy(xhist[:, c, :], xb[:, 512:515]), reads=[xbres], writes=[("xhist", c)])
                    for c4 in cs4:
                        c, j = cg * 4 + c4, c4 % 2
                        xb, xbres = xbt[j], "xbt%d" % j
                        x_c, xcres = xc[j], "xc%d" % j
                        P.dve(I_ts(x_c, xb[:, 0:512], lrup[:, c, 0:1], ALU.mult, lrup[:, c, 4:5], ALU.add),
                              reads=[xbres, "lrup"], writes=[xcres])
                        for tap in range(1, 4):
                            P.dve(I_stt(x_c, xb[:, tap:tap + 512], lrup[:, c, tap:tap + 1], x_c,
                                        ALU.mult, ALU.add), reads=[xbres, "lrup", xcres], writes=[xcres])
                        P.act(I_acopy(xcb[j], x_c), reads=[xcres], writes=["xcb%d" % j])
                    gb_ = {}
                    for c4 in cs4:
                        c, j = cg * 4 + c4, c4 % 2
                        br, brres = lbank()
                        bi, bires = lbank()
                        gb_[c4] = (br, brres, bi, bires)
                        P.pe(I_mm(br[:, :], wg4[:, 0, c, :], xcb[j], True, True), reads=["wg", "xcb%d" % j],
                             writes=[brres])
                        P.pe(I_mm(bi[:, :], wg4[:, 1, c, :], xcb[j], True, True), reads=["wg", "xcb%d" % j],
                             writes=[bires])
                    for c4 in cs4:
                        c, j = cg * 4 + c4, c4 % 2
                        br, brres, bi, bires = gb_[c4]
                        P.act(I_act(r_[j], br[:, :], AF.Sigmoid, bias=lrup[:, c, 5:6]), reads=[brres, "lrup"],
                              writes=["r_%d" % j])
                        P.act(I_act(i_[j], bi[:, :], AF.Sigmoid, bias=lrup[:, c, 6:7]), reads=[bires, "lrup"],
                              writes=["i_%d" % j])
                    for c4 in cs4:
                        c, j = cg * 4 + c4, c4 % 2
                        P.act(I_act(a_[j], r_[j], AF.Exp, scale=cl[:, c:c + 1]), reads=["r_%d" % j, "cl"],
                              writes=["a_%d" % j])
                        P.act(I_act(s_[j], r_[j], AF.Exp, scale=cl2[:, c:c + 1]), reads=["r_%d" % j, "cl"],
                              writes=["s_%d" % j])
                    for c4 in cs4:
                        j = c4 % 2
                        P.act(I_act(s_[j], s_[j], AF.Sqrt, scale=-1.0, bias=1.0), reads=["s_%d" % j],
                              writes=["s_%d" % j])
                    for c4 in cs4:
                        c, j = cg * 4 + c4, c4 % 2
                        P.dve(I_tt(i_[j], i_[j], xc[j], ALU.mult), reads=["i_%d" % j, "xc%d" % j],
                              writes=["i_%d" % j])
                        P.dve(I_tt(i_[j], i_[j], s_[j], ALU.mult), reads=["i_%d" % j, "s_%d" % j],
                              writes=["i_%d" % j])
                        hq, hres = hseq[c4], "hseq%d" % c4
                        P.dve(lambda e, hq=hq, c=c, j=j: e.tensor_tensor_scan(
                            out=hq, data0=a_[j], data1=i_[j], initial=lstate[:, c:c + 1],
                            op0=ALU.mult, op1=ALU.add), reads=["a_%d" % j, "i_%d" % j, ("lstate", c)],
                            writes=[hres])
                        P.act(I_acopy(lstate[:, c:c + 1], hq[:, 511:512]), reads=[hres],
                              writes=[("lstate", c)])
                else:
                    for c4 in cs4:
                        j = c4 % 2
                        bank, bres = banks[c4]
                        P.act(I_act(gy[j], bank[:, :], AF.Gelu_apprx_tanh), reads=[bres], writes=["gy%d" % j])
                    for c4 in cs4:
                        j = c4 % 2
                        hq, hres = hseq[c4], "hseq%d" % c4
                        P.dve(I_tt(y_[j], hq, gy[j], ALU.mult), reads=[hres, "gy%d" % j], writes=["y%d" % j])
                    for c4 in cs4:
                        c, j = cg * 4 + c4, c4 % 2
                        P.act(I_act(ysq[j], y_[j], AF.Square), reads=["y%d" % j], writes=["ysq%d" % j])
                        P.act(I_act(featT[:, 8 + c, :], y_[j], AF.Copy, scale=lrup[:, c, 8:9]),
                              reads=["y%d" % j, "lrup"], writes=[("featT", "l%d" % c)])
                    for c4 in cs4:
                        c, j = cg * 4 + c4, c4 % 2
                        P.pe(I_mm(SC[0:1, :], ones_bf[:, 0:1], ysq[j], c == 0, c == 7),
                             reads=["ones_bf", "ysq%d" % j], writes=["psf5"])

        stream(groups, body, (w_out, 0) if full else None)
        if full:
            P.act(I_acopy(ssrow[0:1, :], SC[0:1, :]), reads=["psf5"], writes=["y0"])
            for n in range(4):
                P.pe(I_mm(SA[:, n:n + 1], ssrow[0:1, n * 128:(n + 1) * 128], one_f[0:1, 0:1], True, True),
                     reads=["y0", "one_f"], writes=["psf3"])
            s, sres = newstat()
            P.act(I_act(s[:, 0:4], SA[:, 0:4], AF.Sqrt, scale=1.0 / 1024, bias=NORM_EPS),
                  reads=["psf3"], writes=[sres])
            P.dve(I_recip(rstd_lru[:, 0:4], s[:, 0:4]), reads=[sres], writes=["rstd_lru"])
            if st == 2:
                dump("featT_lru", featT[:, 8:16, :], "featT", [128, 8, 512], BF16)
                dump("rstd_lru", rstd_lru, "rstd_lru", [128, 8])

    def apply_flag():
        P.fence()
        sf = statef.rearrange("p a b -> p (a b)")
        P.dve(I_ts(sf, sf, flag[:, 0:1], ALU.mult), reads=["statef", "flag"], writes=["statef"])
        for hd in range(8):
            P.act(I_acopy(stateb[sb_par[hd]][:, hd, :], statef[:, hd, :]), reads=["statef"],
                  writes=[("stateb%d" % sb_par[hd], hd)])
        P.dve(I_ts(lstate, lstate, flag[:, 0:1], ALU.mult), reads=["lstate", "flag"], writes=["lstate"])
        xh = xhist.rearrange("p a b -> p (a b)")
        P.dve(I_ts(xh, xh, flag[:, 0:1], ALU.mult), reads=["xhist", "flag"], writes=["xhist"])

    def phase_wout(st):
        P.fence()
        A.top = mark_dense
        for n in range(4):
            r0 = st * 512 + n * 128
            P.dma("sp", I_dma(h[:, n, :], xw[r0:r0 + 128, :]), writes=[("h", n)])
        groups = [(w_out, cg * 512) for cg in range(4)]

        def body(cg, wb, wbres):
            cs_ = slice(cg * 512, (cg + 1) * 512)
            for n in range(4):
                ba, bares = mmbank()
                for kc in range(8):
                    P.pe(I_mm(ba[:, :], featT[:, kc, n * 128:(n + 1) * 128], wb[:, kc, :], kc == 0, kc == 7),
                         reads=["featT", wbres], writes=[bares])
                bb, bbres = mmbank()
                for kc in range(8, 16):
                    P.pe(I_mm(bb[:, :], featT[:, kc, n * 128:(n + 1) * 128], wb[:, kc, :], kc == 8, kc == 15),
                         reads=["featT", wbres], writes=[bbres])
                P.dve(I_tt(h[:, n, cs_], ba[:, :], h[:, n, cs_], ALU.add), reads=[bares, ("h", n)],
                      writes=[("h", n)])
                P.dve(I_stt(h[:, n, cs_], bb[:, :], rstd_lru[:, n:n + 1], h[:, n, cs_], ALU.mult, ALU.add),
                      reads=[bbres, "rstd_lru", ("h", n)], writes=[("h", n)])

        stream(groups, body, (w_xk, 0) if st == 2 else (w_xq, 0))
        if st == 2:
            dump("h1", h, "h", [128, 4, D])

    def phase_xattn(st):
        P.fence()
        A.top = mark_dense
        if st == 2:
            xs1 = A.alloc([D])
            xs = [xs1, xs1]
            mnT = A.alloc([16, 256], BF16)
        qTh = [A.alloc([4, 512], BF16), A.alloc([4, 512], BF16)]
        pexp = [A.alloc([256]) for _ in range(4)]
        pn = [A.alloc([256], BF16) for _ in range(4)]
        PT = [A.alloc([2, 512], BF16), A.alloc([2, 512], BF16)]
        if st == 2:
          load_gain(g_mem)
          for mb in range(2):
            P.dma("sp", I_dma(xs[mb], memd[mb * 128:(mb + 1) * 128, :]), writes=["xsm"])
            norm_block(xs[mb], "xsm", mb, mnT, "mnT", mb * 128)
        groups = [(w_xk, cg * 512, "k") for cg in range(4)] + [(w_xv, cg * 512, "v") for cg in range(4)]

        def body_kv(gi, wb, wbres):
            kind, cg = groups[gi][2], gi % 4
            if kind == "k":
                for fc in range(4):
                    bank, bres = mmbank()
                    for kc in range(16):
                        P.pe(I_mm(bank[:, 0:256], wb[:, kc, fc * 128:(fc + 1) * 128], mnT[:, kc, :],
                                  kc == 0, kc == 15), reads=["mnT", wbres], writes=[bres])
                    P.act(I_acopy(kTm[:, cg * 4 + fc, :], bank[:, 0:256]), reads=[bres],
                          writes=[("kTm", cg * 4 + fc)])
            else:
                for mb in range(2):
                    bank, bres = mmbank()
                    for kc in range(16):
                        P.pe(I_mm(bank[:, :], mnT[:, kc, mb * 128:(mb + 1) * 128], wb[:, kc, :],
                                  kc == 0, kc == 15), reads=["mnT", wbres], writes=[bres])
                    P.act(I_acopy(vm[:, mb, cg * 512:(cg + 1) * 512], bank[:, :]), reads=[bres],
                          writes=[("vm", cg)])

        if st == 2:
            stream(groups, body_kv, (w_xq, 0))
        load_gain(g_xat)
        norm_multi([(h[:, n, :], ("h", n), n, n * 128) for n in range(4)], actT, "actT")
        groups_q = [(w_xq, hd * 512) for hd in range(4)]
        scale = 512.0 ** -0.5

        def body_q(hd, wb, wbres):
            qT, qres = qTh[hd % 2], "qTh%d" % (hd % 2)
            for fc in range(4):
                bank, bres = mmbank()
                for kc in range(16):
                    P.pe(I_mm(bank[:, :], wb[:, kc, fc * 128:(fc + 1) * 128], actT[:, kc, :],
                              kc == 0, kc == 15), reads=["actT", wbres], writes=[bres])
                P.act(I_act(qT[:, fc, :], bank[:, :], AF.Copy, scale=scale), reads=[bres],
                      writes=[(qres, fc)])
            pt, ptres = PT[hd % 2], "PT%d" % (hd % 2)
            sbk = [(SA, "psf3"), (SB_, "psf4")]
            sts = [newstat() for _ in range(4)]
            for n in range(4):
                bk, bkres = sbk[n // 2]
                cs2 = slice((n % 2) * 256, (n % 2) * 256 + 256)
                for fc in range(4):
                    P.pe(I_mm(bk[:, cs2], qT[:, fc, n * 128:(n + 1) * 128], kTm[:, hd * 4 + fc, :],
                              fc == 0, fc == 3), reads=[qres, ("kTm", hd * 4 + fc)], writes=[bkres])
            for n in range(4):
                bk, bkres = sbk[n // 2]
                cs2 = slice((n % 2) * 256, (n % 2) * 256 + 256)
                s, sres = sts[n]
                P.dve(lambda e, s=s, bk=bk, cs2=cs2: e.reduce_max(out=s[:, 0:1], in_=bk[:, cs2], axis=AX.X),
                      reads=[bkres], writes=[sres])
            for n in range(4):
                s, sres = sts[n]
                P.dve(I_ts(s[:, 1:2], s[:, 0:1], -1.0, ALU.mult), reads=[sres], writes=[sres])
            for n in range(4):
                bk, bkres = sbk[n // 2]
                cs2 = slice((n % 2) * 256, (n % 2) * 256 + 256)
                s, sres = sts[n]
                P.act(I_act(pexp[n], bk[:, cs2], AF.Exp, bias=s[:, 1:2], accum_out=s[:, 2:3]),
                      reads=[bkres, sres], writes=["pexp%d" % n, sres])
            for n in range(4):
                s, sres = sts[n]
                P.dve(I_recip(s[:, 3:4], s[:, 2:3]), reads=[sres], writes=[sres])
            for n in range(4):
                s, sres = sts[n]
                P.dve(I_ts(pn[n], pexp[n], s[:, 3:4], ALU.mult), reads=["pexp%d" % n, sres],
                      writes=["pn%d" % n])
            for pr in range(2):
                tb, tbres = tbank()
                for nl in range(2):
                    n = 2 * pr + nl
                    for mb in range(2):
                        o2 = (nl * 2 + mb) * 128
                        P.pe(I_tr(tb[:, o2:o2 + 128], pn[n][:, mb * 128:(mb + 1) * 128], ident),
                             reads=["pn%d" % n, "ident"], writes=[tbres])
                for nl in range(2):
                    n = 2 * pr + nl
                    P.act(I_acopy(pt[:, :, n * 128:(n + 1) * 128],
                                  tb[:, nl * 256:nl * 256 + 256].rearrange("p (a b) -> p a b", a=2)),
                          reads=[tbres], writes=[(ptres, n)])
            for dc in range(4):
                bank, bres = mmbank()
                for mb in range(2):
                    P.pe(I_mm(bank[:, :], vm[:, mb, hd * 512 + dc * 128:hd * 512 + (dc + 1) * 128],
                              pt[:, mb, :], mb == 0, mb == 1), reads=[("vm", hd), ptres], writes=[bres])
                P.act(I_acopy(featT[:, hd * 4 + dc, :], bank[:, :]), reads=[bres],
                      writes=[("featT", hd * 4 + dc)])

        stream(groups_q, body_q, (w_xo, 0))
        groups_o = [(w_xo, cg * 512) for cg in range(4)]

        def body_o(cg, wb, wbres):
            cs_ = slice(cg * 512, (cg + 1) * 512)
            for n in range(4):
                bank, bres = mmbank()
                for kc in range(16):
                    P.pe(I_mm(bank[:, :], featT[:, kc, n * 128:(n + 1) * 128], wb[:, kc, :], kc == 0, kc == 15),
                         reads=["featT", wbres], writes=[bres])
                P.dve(I_tt(h[:, n, cs_], bank[:, :], h[:, n, cs_], ALU.add), reads=[bres, ("h", n)],
                      writes=[("h", n)])

        stream(groups_o, body_o, (w_pq, 0))
        if st == 2:
            dump("h2", h, "h", [128, 4, D])

    def phase_peersel(st):
        P.fence()
        A.top = mark_dense
        qTp = featT
        if os.environ.get("KDBG"):
            print("conv issued before peersel flush st", st, conv["next"])
        conv_issue(256)
        load_gain(g_ffn)
        norm_multi([(h[:, n, :], ("h", n), n, n * 128) for n in range(4)], actT, "actT")
        groups = [(w_pq, cg * 512) for cg in range(4)]

        def body(cg, wb, wbres):
            for fc in range(4):
                bank, bres = mmbank()
                for kc in range(16):
                    P.pe(I_mm(bank[:, :], wb[:, kc, fc * 128:(fc + 1) * 128], actT[:, kc, :],
                              kc == 0, kc == 15), reads=["actT", wbres], writes=[bres])
                P.act(I_acopy(qTp[:, cg * 4 + fc, :], bank[:, :]), reads=[bres], writes=[("featT", cg * 4 + fc)])

        stream(groups, body)

    def phase_gather(st):
        P.fence()
        A.top = mark_feat
        qTp = featT
        skT = A.alloc([2048], BF16)
        P.dma("pool", I_dma(skT, skTd), writes=["skT"])
        skT3 = skT.rearrange("p (a b) -> p a b", a=16)
        iok = A.alloc([128], I32)
        P.pool(lambda e: e.iota(iok, pattern=[[1, 128]], base=0, channel_multiplier=0), writes=["iok"])
        sc = A.alloc([16, 128])
        scr = A.alloc([2048])
        sc2 = scr.rearrange("p (a b) -> p a b", a=16)
        pk2 = scr.rearrange("p (a b) -> p a b", a=8)
        vals = A.alloc([16, 16])
        idl = A.alloc([16, 16], I32)
        cand = A.alloc([8, 256])
        cid = scr.bitcast(I32).rearrange("p (a b) -> p a b", a=8)
        top = A.alloc([8, 16])
        tsc = A.alloc([8, 16])
        ex = A.alloc([8, 16])
        sci = sc.bitcast(I32)
        vali = vals.bitcast(I32)
        candi = cand.bitcast(I32)
        topi = top.bitcast(I32)
        sbanks = [(SB_, "psf4"), (SC, "psf5")]

        def sel_ops(n):
            ops = []
            for q4 in range(4):
                bank, bres = sbanks[q4 % 2]

                def f(q4=q4, bank=bank, bres=bres):
                    for j in range(4):
                        p16 = q4 * 4 + j
                        P.pe(I_mm(bank[:, j * 128:(j + 1) * 128], qTp[:, p16, n * 128:(n + 1) * 128],
                                  skT3[:, p16, :], True, True), reads=[("featT", p16), "skT"], writes=[bres])
                    P.act(I_acopy(sc[:, q4 * 4:(q4 + 1) * 4, :], bank[:, :].rearrange("p (a b) -> p a b", a=4)),
                          reads=[bres], writes=["sc"])
                ops.append(f)
            ops.append(lambda: P.dve(I_tss(sci, sci, -128, ALU.bitwise_and), reads=["sc"], writes=["sc"]))
            ops.append(lambda: P.dve(I_tt(sci, sci, iok.unsqueeze(1).to_broadcast([128, 16, 128]), ALU.bitwise_or),
                                     reads=["sc", "iok"], writes=["sc"]))
            for p16 in range(16):
                ops.append(lambda p16=p16: P.dve(lambda e: e.max(out=vals[:, p16, 0:8], in_=sc[:, p16, :]),
                                                 reads=["sc"], writes=[("vals", p16)]))
                ops.append(lambda p16=p16: P.dve(lambda e: e.match_replace(
                    out=sc2[:, p16, :], in_to_replace=vals[:, p16, 0:8], in_values=sc[:, p16, :], imm_value=NEG),
                    reads=["sc", ("vals", p16)], writes=["scr"]))
                ops.append(lambda p16=p16: P.dve(lambda e: e.max(out=vals[:, p16, 8:16], in_=sc2[:, p16, :]),
                                                 reads=["scr"], writes=[("vals", p16)]))
            id4 = idl.rearrange("p (h c) k -> p h c k", c=2)
            v4 = vals.rearrange("p (h c) k -> p h c k", c=2)
            c4 = cand.rearrange("p h (a b) -> p h a b", a=16)
            ci4 = cid.rearrange("p h (a b) -> p h a b", a=16)
            ops.append(lambda: P.dve(I_tss(idl, vali, 127, ALU.bitwise_and), reads=["vals"], writes=["idl"]))
            ops.append(lambda: P.dve(I_tss(id4[:, :, 0, :], id4[:, :, 0, :], 7, ALU.logical_shift_left),
                                     reads=["idl"], writes=["idl"]))
            ops.append(lambda: P.dve(I_tt(c4, v4[:, :, 0, :].unsqueeze(3).to_broadcast([128, 8, 16, 16]),
                                          v4[:, :, 1, :].unsqueeze(2).to_broadcast([128, 8, 16, 16]), ALU.add),
                                     reads=["vals"], writes=["cand"]))
            ops.append(lambda: P.dve(I_tt(ci4, id4[:, :, 0, :].unsqueeze(3).to_broadcast([128, 8, 16, 16]),
                                          id4[:, :, 1, :].unsqueeze(2).to_broadcast([128, 8, 16, 16]),
                                          ALU.bitwise_or), reads=["idl"], writes=["scr"]))
            ops.append(lambda: P.dve(I_tss(candi, candi, -16384, ALU.bitwise_and), reads=["cand"], writes=["cand"]))
            ops.append(lambda: P.dve(I_tt(candi, candi, cid, ALU.bitwise_or), reads=["cand", "scr"], writes=["cand"]))
            for hd in range(8):
                ops.append(lambda hd=hd: P.dve(lambda e: e.max(out=top[:, hd, 0:8], in_=cand[:, hd, :]),
                                               reads=["cand"], writes=[("top", hd)]))
                ops.append(lambda hd=hd: P.dve(lambda e: e.match_replace(
                    out=pk2[:, hd, :], in_to_replace=top[:, hd, 0:8], in_values=cand[:, hd, :], imm_value=NEG),
                    reads=["cand", ("top", hd)], writes=["scr"]))
                ops.append(lambda hd=hd: P.dve(lambda e: e.max(out=top[:, hd, 8:16], in_=pk2[:, hd, :]),
                                               reads=["scr"], writes=[("top", hd)]))
            ops.append(lambda: P.dve(I_tss(eid[:, n, :], topi.rearrange("p a b -> p (a b)"), 16383, ALU.bitwise_and),
                                     reads=["top"], writes=[("eid", n)]))
            ops.append(lambda: P.dve(I_tss(tsc.bitcast(I32), topi, -16384, ALU.bitwise_and), reads=["top"],
                                     writes=["tsc"]))
            ops.append(lambda: P.dve(I_tt(tsc, tsc, top[:, :, 0:1].to_broadcast([128, 8, 16]), ALU.subtract),
                                     reads=["tsc", "top"], writes=["tsc"]))
            ops.append(lambda: P.act(I_act(ex, tsc, AF.Exp), reads=["tsc"], writes=["ex"]))

            def fin():
                s, sres = newstat()
                P.dve(lambda e: e.tensor_reduce(out=s[:, 0:8], in_=ex, axis=AX.X, op=ALU.add),
                      reads=["ex"], writes=[sres])
                P.dve(I_recip(s[:, 8:16], s[:, 0:8]), reads=[sres], writes=[sres])
                P.dve(I_tt(gate[:, n, :].rearrange("p (a b) -> p a b", a=8), ex,
                           s[:, 8:16].unsqueeze(2).to_broadcast([128, 8, 16]), ALU.mult),
                      reads=["ex", sres], writes=[("gate", n)])
            ops.append(fin)
            return ops

        NB = 7
        gb = [A.alloc([2 * D], BF16) for _ in range(NB)]
        gfin = A.alloc([D])
        NPR = 2
        prod = junk_bufs
        actv = A.alloc([128])
        ge = A.alloc([128])
        gw = A.alloc([128])
        NDG = 4
        dg = [A.alloc([128], BF16) for _ in range(NDG)]
        P.dma("sp", I_dma(gfin, g_fin), writes=["gfin"])
        gc = [0]
        for f in sel_ops(0):
            f()
        for n in range(4):
            nxt_ops = sel_ops(n + 1) if n + 1 < 4 else []
            n_sel = len(nxt_ops)
            if os.environ.get("KINT", "1") != "1":
                while nxt_ops:
                    nxt_ops.pop(0)()
            LAG = 0
            binfo = {}

            def stage_b(sl):
                i, j, gres = binfo.pop(sl)
                P.act(I_act(gw[:, sl:sl + 1], ge[:, sl:sl + 1], AF.Copy, scale=gate[:, n, sl:sl + 1]),
                      reads=[("ge", sl), ("gate", n)], writes=[("gw", sl)])
                P.act(I_act(dg[j], ident, AF.Copy, scale=gw[:, sl:sl + 1]),
                      reads=["ident", ("gw", sl)], writes=["dg%d" % j])
                for dc in range(4):
                    P.pe(I_mm(psf[dc][:, :], dg[j], gb[i][:, D + dc * 512:D + (dc + 1) * 512],
                              sl == 0, sl == 127), reads=["dg%d" % j, gres], writes=["psf%d" % dc])

            for sl in range(128):
                i = gc[0] % NB
                j = gc[0] % NDG
                jp = gc[0] % NPR
                gc[0] += 1
                gres = "gb%d" % i
                pres = "junkA%d" % jp
                binfo[sl] = (i, j, gres)
                P.dma("pool", lambda e, i=i, n=n, sl=sl: e.indirect_dma_start(
                    out=gb[i], out_offset=None, in_=uv,
                    in_offset=IndirectOffsetOnAxis(ap=eid[:, n, sl:sl + 1], axis=0)),
                    reads=[("eid", n), "uv"], writes=[gres])
                if sl % 2 == 0:
                    P.dve(I_stt(prod[0], gb[i][:, 0:D], 1.0, xn[:, n, :], ALU.mult, ALU.mult,
                                accum_out=actv[:, sl:sl + 1]), reads=[gres, ("xn", n)],
                          writes=[("actv", sl), "junkA0"])
                else:
                    P.dve(I_tt(prod[1], gb[i][:, 0:D], xn[:, n, :], ALU.mult), reads=[gres, ("xn", n)],
                          writes=["junkA1"])
                    P.act(I_act(prod[1], prod[1], AF.Copy, accum_out=actv[:, sl:sl + 1]),
                          reads=["junkA1"], writes=[("actv", sl), "junkA1"])
                P.act(I_act(ge[:, sl:sl + 1], actv[:, sl:sl + 1], AF.Gelu_apprx_tanh),
                      reads=[("actv", sl)], writes=[("ge", sl)])
                if sl >= LAG:
                    stage_b(sl - LAG)
                if sl % 16 == 8:
                    for _ in range(-(-n_sel // 8)):
                        if nxt_ops:
                            nxt_ops.pop(0)()
            for sl in range(128 - LAG, 128):
                stage_b(sl)
            while nxt_ops:
                nxt_ops.pop(0)()
            for dc in range(4):
                cs_ = slice(dc * 512, (dc + 1) * 512)
                P.dve(I_tt(h[:, n, cs_], psf[dc][:, :], h[:, n, cs_], ALU.add),
                      reads=["psf%d" % dc, ("h", n)], writes=[("h", n)])
            s, sres = newstat()
            jk, jkres = junkA()
            P.act(I_act(jk, h[:, n, :], AF.Square, accum_out=s[:, 0:1]), reads=[("h", n)],
                  writes=[sres, jkres])
            P.act(I_act(s[:, 1:2], s[:, 0:1], AF.Sqrt, scale=1.0 / D, bias=NORM_EPS), reads=[sres],
                  writes=[sres])
            P.dve(I_recip(s[:, 2:3], s[:, 1:2]), reads=[sres], writes=[sres])
            io = gc[0] % NB
            gc[0] += 1
            o, ores = gb[io].bitcast(F32), "gb%d" % io
            P.dve(I_stt(o, h[:, n, :], s[:, 2:3], gfin, ALU.mult, ALU.mult),
                  reads=[("h", n), sres, "gfin"], writes=[ores])
            r0 = (st - 2) * 512 + n * 128
            P.dma("sp", I_dma(outd[r0:r0 + 128, :], o), reads=[ores])
        if st == 2:
            dump("eid", eid, "eid", [128, 4, 128], I32)
            dump("gate", gate, "gate", [128, 4, 128])
            dump("h3", h, "h", [128, 4, D])

    phases = []
    for st in range(n_sub):
        full = st >= 2
        if os.environ.get("KDBG"):
            print("conv issued before st", st, conv["next"])
        conv["quota"] = 4 if st < 2 else 3
        conv["rate"] = 0.62 if st < 2 else 0.8
        phase_norm1(st)
        phase_ret(st, full)
        phase_lru(st, full)
        if st == 1:
            apply_flag()
        if full:
            if stop_after == "lru":
                break
            phase_wout(st)
            if stop_after == "wout":
                break
            phase_xattn(st)
            if stop_after == "xattn":
                break
            phase_peersel(st)
            if stop_after == "peersel":
                break
            phase_gather(st)
    stats = P.emit()
    stats["sbuf_peak_words"] = A.peak
    return nc, stats, dbg_out


_LOG_GAMMA = np.log1p(-np.exp2(-5.0 - np.arange(8, dtype=np.float64)))
BD = np.exp(128.0 * _LOG_GAMMA)


def _consts():
    lg = _LOG_GAMMA
    idx = np.arange(128, dtype=np.float64)
    i = idx[None, :]
    j = idx[:, None]
    ci, cj = (i // 64), (j // 64)
    scale = 128.0 ** -0.5
    maskT = np.zeros((128, 8, 128), np.float64)
    for hd in range(8):
        same = np.exp(np.abs(i - j) * lg[hd])
        later = np.exp((i - j) * lg[hd])
        m = np.where(ci == cj, same, np.where(ci > cj, later, 0.0))
        maskT[:, hd, :] = m * scale
    qd = np.exp((idx[:, None] + 1.0) * lg[None, :])
    kd = np.exp((127.0 - idx[:, None]) * lg[None, :]) * scale
    qkd = np.concatenate([qd, kd], axis=1)
    return maskT.astype(np.float32), qkd.astype(np.float32)


def _rope_table(pos0):
    half = 64
    inv_freq = (np.float32(10000.0) ** (-np.arange(half, dtype=np.float32) / np.float32(half))).astype(np.float32)
    pos = (pos0 + np.arange(2048)).astype(np.float32)
    ang = (pos[:, None] * inv_freq[None, :]).astype(np.float32)
    cs = np.concatenate([np.cos(ang), np.sin(ang)], axis=1).astype(np.float32)
    return np.ascontiguousarray(cs.reshape(16, 128, 128).transpose(1, 0, 2))


def _wl(w):
    k, n = w.shape
    return np.ascontiguousarray(w.reshape(k // 128, 128, n).transpose(1, 0, 2))


def _rep(v):
    return np.ascontiguousarray(np.broadcast_to(np.asarray(v, np.float32)[None, :], (128, v.shape[0])))


_CACHE = {}


def make_in_maps(inp):
    f = lambda a: np.asarray(a, dtype=np.float32)
    x = f(inp["x"])
    mem = f(inp["mem"])
    maskT, qkd = _consts()
    shared = {
        "maskT": maskT, "qkd": qkd,
        "g_mix": _rep(f(inp["mix_norm_g"])[0]), "g_xattn": _rep(f(inp["xattn_norm_g"])[0]),
        "g_mem": _rep(f(inp["mem_norm_g"])[0]), "g_ffn": _rep(f(inp["ffn_norm_g"])[0]),
        "g_final": _rep(f(inp["final_norm_g"])), "g_ret": _rep(f(inp["ret_gn_g"])[0]),
        "w_in": _wl(f(inp["w_in"])[0]), "w_out": _wl(f(inp["w_out"])[0]),
        "w_xq": _wl(f(inp["w_xq"])[0]), "w_xk": _wl(f(inp["w_xk"])[0]),
        "w_xv": _wl(f(inp["w_xv"])[0]), "w_xo": _wl(f(inp["w_xo"])[0]),
        "peer_w_q": _wl(f(inp["peer_w_q"])[0]),
        "peer_u": np.ascontiguousarray(f(inp["peer_u"])[0]),
        "peer_v": np.ascontiguousarray(f(inp["peer_v"])[0]),
    }
    cw = f(inp["conv_w"])[0]
    cols = [cw[0], cw[1], cw[2], cw[3], f(inp["conv_b"])[0], f(inp["b_rg"])[0].reshape(-1),
            f(inp["b_ig"])[0].reshape(-1), f(inp["lru_lambda"])[0], f(inp["lru_norm_g"])[0]]
    lrup = np.stack(cols, axis=-1)
    shared["lrup"] = np.ascontiguousarray(lrup.reshape(8, 128, 9).transpose(1, 0, 2))
    wrg = f(inp["w_rg"])[0]
    wig = f(inp["w_ig"])[0]
    wg = np.stack([wrg, wig], axis=0)
    shared["w_gates"] = np.ascontiguousarray(wg.transpose(2, 0, 1, 3).reshape(128, 2048))
    sk = f(inp["peer_sub_keys"])[0]
    shared["skT"] = np.ascontiguousarray(sk.transpose(3, 0, 1, 2).reshape(128, 2048))
    in_maps = []
    for c in range(8):
        b, s = c // 2, c % 2
        xwin = np.zeros((2048, D), np.float32)
        if s == 1:
            xwin[0:1024] = x[b, 0:1024]
        xwin[1024:2048] = x[b, s * 1024:(s + 1) * 1024]
        m = dict(shared)
        m["xw"] = xwin
        m["mem"] = np.ascontiguousarray(mem[b])
        m["flag"] = np.full((128, 1), float(s), np.float32)
        m["cs"] = _rope_table(s * 1024 - 1024)
        in_maps.append(m)
    return in_maps


def kernel(**inputs):
    if "nc" not in _CACHE:
        _CACHE["nc"] = build_program()[0]
    nc = _CACHE["nc"]
    in_maps = make_in_maps(inputs)
    res = run_bass_kernel_spmd(nc, in_maps, core_ids=list(range(8)))
    out = np.zeros((4, 2048, D), np.float32)
    for c in range(8):
        b, s = c // 2, c % 2
        out[b, s * 1024:(s + 1) * 1024] = np.asarray(res.results[c]["out"], np.float32)
    return out
```

```python
import os
import numpy as np
import concourse.bass as bass
import concourse.mybir as mybir
from concourse.bass import IndirectOffsetOnAxis
from concourse.bass_utils import run_bass_kernel_spmd
from contextlib import ExitStack

F32 = mybir.dt.float32
BF16 = mybir.dt.bfloat16
I32 = mybir.dt.int32
AF = mybir.ActivationFunctionType
ALU = mybir.AluOpType
AX = mybir.AxisListType

D = 2048
NORM_EPS = 1e-6
GN_EPS = 1e-5
NEG = -1.0e30


class Prog:
    ENGS = ("pe", "act", "dve", "pool", "sp")

    def __init__(self, nc):
        self.nc = nc
        self.ins = []
        self.state = {}
        self.fence_idx = None
        self.n_dma_sems = {"sp": 24, "pool": 48}

    @staticmethod
    def _norm(r):
        return r if isinstance(r, tuple) else (r, None)

    def _deps_for(self, res, is_write, out):
        name, key = res
        st = self.state.setdefault(name, {"W": {}, "R": {}})
        if key is None:
            keys = set(st["W"].keys()) | set(st["R"].keys())
        else:
            keys = (key, None)
        for k in keys:
            if k in st["W"]:
                i = st["W"][k]
                out[i] = max(out.get(i, 0), 2 if not is_write else 1)
            if is_write:
                for i in st["R"].get(k, ()):
                    out[i] = max(out.get(i, 0), 1)

    def _commit(self, res, is_write, idx):
        name, key = res
        st = self.state[name]
        if is_write:
            if key is None:
                st["W"] = {None: idx}
                st["R"] = {}
            else:
                st["W"][key] = idx
                st["R"][key] = []
        else:
            st["R"].setdefault(key, []).append(idx)

    def op(self, eng, fn, reads=(), writes=(), dma=False):
        reads = [self._norm(r) for r in reads]
        writes = [self._norm(w) for w in writes]
        idx = len(self.ins)
        deps = {}
        for r in reads:
            self._deps_for(r, False, deps)
        for w in writes:
            self._deps_for(w, True, deps)
        for r in reads:
            self._commit(r, False, idx)
        for w in writes:
            self._commit(w, True, idx)
        deps.pop(idx, None)
        if self.fence_idx is not None:
            deps[self.fence_idx] = 2
        self.ins.append(dict(eng=eng, fn=fn, deps=deps, dma=dma, signal=dma))
        return idx

    def pe(self, fn, reads=(), writes=()):
        return self.op("pe", fn, reads, writes)

    def act(self, fn, reads=(), writes=()):
        return self.op("act", fn, reads, writes)

    def dve(self, fn, reads=(), writes=()):
        return self.op("dve", fn, reads, writes)

    def pool(self, fn, reads=(), writes=()):
        return self.op("pool", fn, reads, writes)

    def dma(self, q, fn, reads=(), writes=()):
        return self.op(q, fn, reads, writes, dma=True)

    def fence(self):
        deps = {}
        for name, st in self.state.items():
            for i in st["W"].values():
                deps[i] = 2
            for l in st["R"].values():
                for i in l:
                    deps[i] = 2
        if self.fence_idx is not None:
            deps[self.fence_idx] = 2
        idx = len(self.ins)
        self.ins.append(dict(eng="pool", fn=lambda e: e.nop(), deps=deps, dma=False,
                             signal=True, fence=True))
        self.state = {}
        self.fence_idx = idx

    def emit(self):
        nc = self.nc
        ins = self.ins
        for I in ins:
            real = []
            for d, kind in I["deps"].items():
                J = ins[d]
                same = (J["eng"] == I["eng"]) and not J["dma"] and not I["dma"]
                if same and not J.get("fence") and not I.get("fence"):
                    if I["eng"] == "pe":
                        continue
                real.append(d)
            best = {}
            keep = []
            for d in real:
                J = ins[d]
                if J["dma"]:
                    keep.append(d)
                else:
                    if best.get(J["eng"], -1) < d:
                        best[J["eng"]] = d
            real = keep + list(best.values())
            I["rdeps"] = real
            for d in real:
                ins[d]["signal"] = True
        with ExitStack() as es:
            eng_sem = {e: es.enter_context(nc.semaphore("s_" + e)) for e in self.ENGS}
            dma_sems = {q: [es.enter_context(nc.semaphore("d_%s%d" % (q, i)))
                            for i in range(n)] for q, n in self.n_dma_sems.items()}
            eng_cnt = {e: 0 for e in self.ENGS}
            dma_cnt = {q: 0 for q in self.n_dma_sems}
            for I in ins:
                if I["dma"]:
                    q = I["eng"]
                    k = dma_cnt[q]
                    dma_cnt[q] += 1
                    n = len(dma_sems[q])
                    I["sem"] = dma_sems[q][k % n]
                    I["val"] = 16 * (k // n + 1)
                    I["semkey"] = (q, k % n)
                elif I["signal"]:
                    e = I["eng"]
                    eng_cnt[e] += 1
                    I["sem"] = eng_sem[e]
                    I["val"] = eng_cnt[e]
                    I["semkey"] = e
            per_eng = {e: [] for e in self.ENGS}
            for I in ins:
                per_eng[I["eng"]].append(I)
            final_dma = {}
            for I in ins:
                if I["dma"]:
                    final_dma[I["semkey"]] = (I["sem"], I["val"])
            nwaits = {e: 0 for e in self.ENGS}
            block = es.enter_context(nc.Block())

            def run(engname, eng):
                waited = {}
                for I in per_eng[engname]:
                    need = {}
                    for d in I["rdeps"]:
                        J = ins[d]
                        sk = J["semkey"]
                        if need.get(sk, (None, 0))[1] < J["val"]:
                            need[sk] = (J["sem"], J["val"])
                    if I["dma"] and I["val"] > 16:
                        sk = I["semkey"]
                        if need.get(sk, (None, 0))[1] < I["val"] - 16:
                            need[sk] = (I["sem"], I["val"] - 16)
                    for sk, (sem, val) in need.items():
                        if waited.get(sk, 0) >= val:
                            continue
                        eng.wait_ge(sem, val)
                        nwaits[engname] += 1
                        waited[sk] = val
                    r = I["fn"](eng)
                    if I["dma"]:
                        r.then_inc(I["sem"], 16)
                    elif I["signal"]:
                        r.then_inc(I["sem"], 1)
                if engname == "sp":
                    for sk, (sem, val) in final_dma.items():
                        if waited.get(sk, 0) < val:
                            eng.wait_ge(sem, val)

            @block.tensor
            def _(e):
                run("pe", e)

            @block.scalar
            def _(e):
                run("act", e)

            @block.vector
            def _(e):
                run("dve", e)

            @block.gpsimd
            def _(e):
                run("pool", e)

            @block.sync
            def _(e):
                run("sp", e)
        return {"n_ins": len(ins), "eng_cnt": eng_cnt, "dma_cnt": dma_cnt, "nwaits": nwaits}


def I_act(out, in_, func, **kw):
    return lambda e: e.activation(out=out, in_=in_, func=func, **kw)


def I_tt(out, a, b, op):
    return lambda e: e.tensor_tensor(out=out, in0=a, in1=b, op=op)


def I_ts(out, a, s1, op0, s2=None, op1=None):
    if op1 is None:
        return lambda e: e.tensor_scalar(out=out, in0=a, scalar1=s1, scalar2=None, op0=op0)
    return lambda e: e.tensor_scalar(out=out, in0=a, scalar1=s1, scalar2=s2, op0=op0, op1=op1)


def I_tss(out, a, s, op):
    return lambda e: e.tensor_single_scalar(out=out, in_=a, scalar=s, op=op)


def I_stt(out, a, s, b, op0, op1, **kw):
    return lambda e: e.scalar_tensor_tensor(out=out, in0=a, scalar=s, in1=b, op0=op0, op1=op1, **kw)


def I_mm(out, lhsT, rhs, start, stop):
    return lambda e: e.matmul(out, lhsT=lhsT, rhs=rhs, start=start, stop=stop)


def I_tr(out, in_, ident):
    return lambda e: e.transpose(out=out, in_=in_, identity=ident)


def I_acopy(out, in_):
    return lambda e: e.copy(out=out, in_=in_)


def I_copy(out, in_):
    return lambda e: e.tensor_copy(out=out, in_=in_)


def I_dma(out, in_):
    return lambda e: e.dma_start(out=out, in_=in_)


def I_recip(out, in_):
    return lambda e: e.reciprocal(out=out, in_=in_)


def I_memset(ap, v):
    return lambda e: e.memset(ap, v)


class Arena:
    def __init__(self, nc, words):
        self.t = nc.alloc_sbuf_tensor("arena", [128, words], F32)
        self.words = words
        self.top = 0
        self.peak = 0

    def alloc(self, shape, dt=F32):
        shape = list(shape)
        n = int(np.prod(shape))
        esz = 2 if dt == BF16 else 4
        words = (n * esz + 3) // 4
        words = (words + 7) // 8 * 8
        off = self.top
        self.top += words
        assert self.top <= self.words, ("SBUF arena overflow", self.top, self.words)
        self.peak = max(self.peak, self.top)
        v = self.t[:, off:off + words]
        if dt != F32:
            v = v.bitcast(dt)
        v = v[:, 0:n]
        if len(shape) == 2:
            v = v.rearrange("p (a b) -> p a b", a=shape[0])
        elif len(shape) == 3:
            v = v.rearrange("p (a b c) -> p a b c", a=shape[0], b=shape[1])
        return v


def build_program(dbg=None, n_sub=4, stop_after=None):
    nc = bass.Bass("TRN2", target_bir_lowering=False)
    P = Prog(nc)
    dbg = dbg or {}
    dbg_out = {}

    def din(name, shape, dt=F32):
        return nc.dram_tensor(name, list(shape), dt, kind="ExternalInput").ap()

    xw = din("xw", [2048, D])
    memd = din("mem", [256, D])
    flagd = din("flag", [128, 1])
    csd = din("cs", [128, 16, 128])
    maskd = din("maskT", [128, 8, 128])
    qkdd = din("qkd", [128, 16])
    g_mix = din("g_mix", [128, D])
    g_xat = din("g_xattn", [128, D])
    g_mem = din("g_mem", [128, D])
    g_ffn = din("g_ffn", [128, D])
    g_fin = din("g_final", [128, D])
    gretd = din("g_ret", [128, 1024])
    lrupd = din("lrup", [128, 8, 9])
    wgd = din("w_gates", [128, 2048])
    w_in = din("w_in", [128, 16, 6144])
    w_out = din("w_out", [128, 16, D])
    w_xq = din("w_xq", [128, 16, D])
    w_xk = din("w_xk", [128, 16, D])
    w_xv = din("w_xv", [128, 16, D])
    w_xo = din("w_xo", [128, 16, D])
    w_pq = din("peer_w_q", [128, 16, D])
    skTd = din("skT", [128, 2048])
    pu = din("peer_u", [16384, D])
    pv = din("peer_v", [16384, D])
    outd = nc.dram_tensor("out", [1024, D], F32, kind="ExternalOutput").ap()
    uv = nc.dram_tensor("uv_bf16", [16384, 2 * D], BF16, kind="Internal").ap()

    def dump(name, ap, res, shape, dt=F32):
        if name not in dbg:
            return
        o = nc.dram_tensor("dbg_" + name, list(shape), dt, kind="ExternalOutput").ap()
        dbg_out[name] = o
        P.dma("sp", I_dma(o, ap), reads=[res])

    psf = [nc.alloc_psum_tensor("psf%d" % i, [128, 512], F32) for i in range(6)]
    psb = [nc.alloc_psum_tensor("psb%d" % i, [128, 1024], BF16) for i in range(2)]
    mm_ctr = [0]

    def mmbank():
        i = mm_ctr[0] % 3
        mm_ctr[0] += 1
        return psf[i], "psf%d" % i

    tb_ctr = [0]

    def tbank():
        i = tb_ctr[0] % 2
        tb_ctr[0] += 1
        return psb[i], "psb%d" % i

    SA, SB_, SC = psf[3], psf[4], psf[5]
    slot_ctr = {"psf3": 0, "psf4": 0, "psf5": 0}

    def slot(bank, name, w=128):
        ns = 512 // w
        s = slot_ctr[name] % ns
        slot_ctr[name] += 1
        return bank[:, s * w:(s + 1) * w], (name, "%d_%d" % (w, s)) if False else (name, None)


    A = Arena(nc, 53200)
    ident = A.alloc([128], BF16)
    ones_bf = A.alloc([8], BF16)
    one_f = A.alloc([8])
    qkd = A.alloc([16])
    flag = A.alloc([8])
    statef = A.alloc([8, 128])
    stateb = [A.alloc([8, 128], BF16), A.alloc([8, 128], BF16)]
    lstate = A.alloc([8])
    xhist = A.alloc([8, 3])
    eid = A.alloc([4, 128], I32)
    gate = A.alloc([4, 128])
    rstd_lru = A.alloc([8])
    stat = A.alloc([32, 16])
    h = A.alloc([4, D])
    xn = A.alloc([4, D], BF16)
    junk_bufs = [A.alloc([D], BF16), A.alloc([D], BF16)]
    junk_ctr = [0]

    def junkA():
        i = junk_ctr[0] % 2
        junk_ctr[0] += 1
        return junk_bufs[i], "junkA%d" % i
    cb = [None] * 4
    kTm = A.alloc([16, 256], BF16)
    vm = A.alloc([2, D], BF16)
    mark_always = A.top
    featT = A.alloc([16, 512], BF16)
    mark_feat = A.top
    actT = A.alloc([16, 512], BF16)
    wbuf = [A.alloc([16, 512], BF16), A.alloc([16, 512], BF16)]
    gbuf = A.alloc([D])
    mark_dense = A.top

    stat_ctr = [0]

    def newstat():
        i = stat_ctr[0] % 32
        stat_ctr[0] += 1
        return stat[:, i, :], ("stat", i)

    P.pool(I_memset(ident, 1.0), writes=["ident"])
    P.pool(lambda e: e.affine_select(out=ident, in_=ident, pattern=[[-1, 128]],
                                     compare_op=ALU.is_equal, fill=0.0, base=0,
                                     channel_multiplier=1), reads=["ident"], writes=["ident"])
    P.pool(I_memset(ones_bf, 1.0), writes=["ones_bf"])
    P.pool(I_memset(one_f, 1.0), writes=["one_f"])
    P.pool(I_memset(statef, 0.0), writes=["statef"])
    P.pool(I_memset(stateb[0], 0.0), writes=["stateb0"])
    P.pool(I_memset(stateb[1], 0.0), writes=["stateb1"])
    P.pool(I_memset(lstate, 0.0), writes=["lstate"])
    P.pool(I_memset(xhist, 0.0), writes=["xhist"])
    P.dma("sp", I_dma(qkd, qkdd), writes=["qkd"])
    P.dma("sp", I_dma(flag[:, 0:1], flagd), writes=["flag"])

    sb_par = [0] * 8
    w_ctr = [0]

    conv = {"next": 0, "quota": 4, "rate": 0.65}

    def conv_issue(k):
        if k < 256:
            conv["acc"] = conv.get("acc", 0.0) + k * conv["rate"]
            k = int(conv["acc"])
            conv["acc"] -= k
        while k > 0 and conv["next"] < 256:
            t = conv["next"]
            conv["next"] += 1
            k -= 1
            tile_i, half = t // 2, t % 2
            src = pu if half == 0 else pv
            r0 = tile_i * 128
            if os.environ.get("KD2D", "1") == "1":
                P.dma("pool", I_dma(uv[r0:r0 + 128, half * D:(half + 1) * D], src[r0:r0 + 128, :]),
                      writes=[("uv", t)])
                continue
            j = t % 4
            P.dma("pool", I_dma(cb[j], src[r0:r0 + 128, :]), writes=["cb%d" % j])
            P.dma("sp", I_dma(uv[r0:r0 + 128, half * D:(half + 1) * D], cb[j]), reads=["cb%d" % j],
                  writes=[("uv", t)])

    def load_w(wd, col0):
        i = w_ctr[0] % 2
        w_ctr[0] += 1
        res = "wbuf%d" % i
        P.dma("pool", I_dma(wbuf[i], wd[:, :, col0:col0 + 512]), writes=[res])
        conv_issue(conv["quota"])
        return wbuf[i], res

    pend = {}

    def stream(groups, body, nxt=None):
        bufs = [None] * len(groups)
        key0 = (id(groups[0][0]), groups[0][1])
        if pend.get("key") == key0:
            bufs[0] = pend["buf"]
        else:
            bufs[0] = load_w(*groups[0][:2])
        pend.clear()
        for i in range(len(groups)):
            if i + 1 < len(groups):
                bufs[i + 1] = load_w(*groups[i + 1][:2])
            elif nxt is not None and os.environ.get("KPREF", "1") == "1":
                pend["key"] = (id(nxt[0]), nxt[1])
                pend["buf"] = load_w(*nxt)
            body(i, bufs[i][0], bufs[i][1])

    def load_gain(gd):
        P.dma("sp", I_dma(gbuf, gd), writes=["gbuf"])

    def norm_block(src, src_res, n, dstT=None, dstT_res=None, col0=0, keep_rstd=None):
        s, sres = newstat()
        jk, jkres = junkA()
        P.act(I_act(jk, src, AF.Square, accum_out=s[:, 0:1]), reads=[src_res], writes=[sres, jkres])
        P.act(I_act(s[:, 1:2], s[:, 0:1], AF.Sqrt, scale=1.0 / D, bias=NORM_EPS),
              reads=[sres], writes=[sres])
        P.dve(I_recip(s[:, 2:3], s[:, 1:2]), reads=[sres], writes=[sres])
        P.dve(I_stt(xn[:, n, :], src, s[:, 2:3], gbuf, ALU.mult, ALU.mult),
              reads=[src_res, sres, "gbuf"], writes=[("xn", n)])
        if dstT is None:
            return s, sres
        for half in range(2):
            tb, tbres = tbank()
            for j in range(8):
                kc = half * 8 + j
                P.pe(I_tr(tb[:, j * 128:(j + 1) * 128], xn[:, n, kc * 128:(kc + 1) * 128], ident),
                     reads=[("xn", n), "ident"], writes=[tbres])
            P.act(I_acopy(dstT[:, half * 8:(half + 1) * 8, col0:col0 + 128],
                          tb[:, :].rearrange("p (a b) -> p a b", a=8)),
                  reads=[tbres], writes=[(dstT_res, col0 // 128)])
        return s, sres

    def norm_multi(items, dstT, dstT_res):
        st_ = []
        for (src, src_res, n, col0) in items:
            s, sres = newstat()
            jk, jkres = junkA()
            P.act(I_act(jk, src, AF.Square, accum_out=s[:, 0:1]), reads=[src_res], writes=[sres, jkres])
            st_.append((s, sres))
        for (s, sres) in st_:
            P.act(I_act(s[:, 1:2], s[:, 0:1], AF.Sqrt, scale=1.0 / D, bias=NORM_EPS),
                  reads=[sres], writes=[sres])
        for (s, sres) in st_:
            P.dve(I_recip(s[:, 2:3], s[:, 1:2]), reads=[sres], writes=[sres])
        for (src, src_res, n, col0), (s, sres) in zip(items, st_):
            P.dve(I_stt(xn[:, n, :], src, s[:, 2:3], gbuf, ALU.mult, ALU.mult),
                  reads=[src_res, sres, "gbuf"], writes=[("xn", n)])
        for (src, src_res, n, col0) in items:
            for half in range(2):
                tb, tbres = tbank()
                for j in range(8):
                    kc = half * 8 + j
                    P.pe(I_tr(tb[:, j * 128:(j + 1) * 128], xn[:, n, kc * 128:(kc + 1) * 128], ident),
                         reads=[("xn", n), "ident"], writes=[tbres])
                P.act(I_acopy(dstT[:, half * 8:(half + 1) * 8, col0:col0 + 128],
                              tb[:, :].rearrange("p (a b) -> p a b", a=8)),
                      reads=[tbres], writes=[(dstT_res, col0 // 128)])

    def phase_norm1(st):
        P.fence()
        A.top = mark_dense
        xs = [A.alloc([D]), A.alloc([D])]
        load_gain(g_mix)
        for pr in range(2):
            items = []
            for n in (2 * pr, 2 * pr + 1):
                r0 = st * 512 + n * 128
                P.dma("sp", I_dma(xs[n % 2], xw[r0:r0 + 128, :]), writes=["xs%d" % (n % 2)])
                conv_issue(3)
                items.append((xs[n % 2], "xs%d" % (n % 2), n, n * 128))
            norm_multi(items, actT, "actT")

    def phase_ret(st, full):
        P.fence()
        A.top = mark_dense
        cs_l = A.alloc([4, 128])
        P.dma("sp", I_dma(cs_l, csd[:, st * 4:(st + 1) * 4, :]), writes=["cs_l"])
        if full:
            maskT = A.alloc([8, 128])
            gret = A.alloc([1024])
            P.dma("sp", I_dma(maskT, maskd), writes=["maskT"])
            P.dma("sp", I_dma(gret, gretd), writes=["gret"])
        qrot = A.alloc([4, 512], BF16)
        krot = A.alloc([4, 512], BF16)
        vv = A.alloc([4, 512], BF16)
        sg = A.alloc([4, 512], BF16)
        qraw1 = A.alloc([512])
        qraw = [qraw1, qraw1]
        ta = A.alloc([4, 64])
        tb_ = A.alloc([4, 64])
        tc = ta
        td = tb_
        kt = [A.alloc([4, 128], BF16), A.alloc([4, 128], BF16)]
        qt = [A.alloc([4, 128], BF16), A.alloc([4, 128], BF16)]
        qkT = [A.alloc([6, 128], BF16) for _ in range(4)]
        pT = [A.alloc([128], BF16) for _ in range(8)]
        o_sb1 = A.alloc([4, 128])
        o_sb = [o_sb1, o_sb1]
        yn = A.alloc([4, 128])
        retb = [A.alloc([512], BF16), A.alloc([512], BF16)]
        raw_ctr = [0]
        it = [0]
        for hh in range(2):
            kinds = ["q", "k", "v", "g"] if full else ["k", "v"]
            base = {"q": 0, "k": 1024, "v": 2048, "g": 3072}
            groups = [(w_in, base[kd] + hh * 512, kd) for kd in kinds]

            def body(gi, wb, wbres, groups=groups):
                kind = groups[gi][2]
                for n in range(4):
                    conv_issue(1)
                    bank, bres = mmbank()
                    for kc in range(16):
                        P.pe(I_mm(bank[:, :], actT[:, kc, n * 128:(n + 1) * 128], wb[:, kc, :],
                                  kc == 0, kc == 15), reads=[("actT", n), wbres], writes=[bres])
                    if kind in ("q", "k"):
                        dst, dres = (qrot, "qrot") if kind == "q" else (krot, "krot")
                        ri = raw_ctr[0] % 2
                        raw_ctr[0] += 1
                        raw, rres = qraw[ri], "qraw0"
                        P.act(I_acopy(raw, bank[:, :]), reads=[bres], writes=[rres])
                        r4 = raw.rearrange("p (h t d) -> p h t d", h=4, t=2)
                        r1, r2 = r4[:, :, 0, :], r4[:, :, 1, :]
                        cosb = cs_l[:, n, 0:64].unsqueeze(1).to_broadcast([128, 4, 64])
                        sinb = cs_l[:, n, 64:128].unsqueeze(1).to_broadcast([128, 4, 64])
                        d4 = dst[:, n, :].rearrange("p (h t d) -> p h t d", h=4, t=2)
                        P.dve(I_tt(ta, r1, cosb, ALU.mult), reads=[rres, "cs_l"], writes=["ta"])
                        P.dve(I_tt(tb_, r2, sinb, ALU.mult), reads=[rres, "cs_l"], writes=["tb"])
                        P.dve(I_tt(d4[:, :, 0, :], ta, tb_, ALU.subtract), reads=["ta", "tb"],
                              writes=[(dres, n)])
                        P.dve(I_tt(tc, r1, sinb, ALU.mult), reads=[rres, "cs_l"], writes=["ta"])
                        P.dve(I_tt(td, r2, cosb, ALU.mult), reads=[rres, "cs_l"], writes=["tb"])
                        P.dve(I_tt(d4[:, :, 1, :], tc, td, ALU.add), reads=["ta", "tb"],
                              writes=[(dres, n)])
                    elif kind == "v":
                        P.act(I_acopy(vv[:, n, :], bank[:, :]), reads=[bres], writes=[("vv", n)])
                    else:
                        P.act(I_act(sg[:, n, :], bank[:, :], AF.Silu), reads=[bres], writes=[("sg", n)])

            if hh == 0:
                nxt = (w_in, (0 if full else 1024) + 512)
            else:
                nxt = (w_in, 4096)
            stream(groups, body, nxt)

            H4 = range(4)
            hsl = [slice(h4 * 128, (h4 + 1) * 128) for h4 in H4]
            ctx = {}

            def front(n, hh=hh):
                c = {}
                ctx[n] = c
                conv_issue(2)
                c["ktb"], c["ktres"] = kt[n % 2], "kt%d" % (n % 2)
                P.dve(I_tt(c["ktb"], krot[:, n, :].rearrange("p (h d) -> p h d", h=4),
                           qkd[:, 8 + hh * 4:12 + hh * 4].unsqueeze(2).to_broadcast([128, 4, 128]),
                           ALU.mult), reads=[("krot", n), "qkd"], writes=[c["ktres"]])
                if not full:
                    return
                qtb, qtres = qt[n % 2], "qt%d" % (n % 2)
                P.dve(I_tt(qtb, qrot[:, n, :].rearrange("p (h d) -> p h d", h=4),
                           qkd[:, hh * 4:hh * 4 + 4].unsqueeze(2).to_broadcast([128, 4, 128]),
                           ALU.mult), reads=[("qrot", n), "qkd"], writes=[qtres])
                qks = []
                for pr in range(2):
                    tb, tbres = tbank()
                    for jj, h4 in enumerate((2 * pr, 2 * pr + 1)):
                        o3 = jj * 384
                        P.pe(I_tr(tb[:, o3:o3 + 128], qrot[:, n, hsl[h4]], ident),
                             reads=[("qrot", n), "ident"], writes=[tbres])
                        P.pe(I_tr(tb[:, o3 + 128:o3 + 256], qtb[:, h4, :], ident),
                             reads=[qtres, "ident"], writes=[tbres])
                        P.pe(I_tr(tb[:, o3 + 256:o3 + 384], krot[:, n, hsl[h4]], ident),
                             reads=[("krot", n), "ident"], writes=[tbres])
                    qi = (n % 2) * 2 + pr
                    qk2, qk2res = qkT[qi], "qkT%d" % qi
                    P.act(I_acopy(qk2, tb[:, 0:768].rearrange("p (a b) -> p a b", a=6)),
                          reads=[tbres], writes=[qk2res])
                    qks.append((qk2, qk2res))
                c["qks"] = qks
                for h4 in H4:
                    qT_, qtT_, kT_, qkres = qk_of(c, h4)
                    P.pe(I_mm(SA[:, hsl[h4]], kT_, qT_, True, True), reads=[qkres], writes=["psf3"])
                for h4 in H4:
                    hd = hh * 4 + h4
                    pi = (n % 2) * 4 + h4
                    P.dve(I_tt(pT[pi], SA[:, hsl[h4]], maskT[:, hd, :], ALU.mult),
                          reads=["psf3", "maskT"], writes=["pT%d" % (n % 2)])

            def qk_of(c, h4):
                qk2, qk2res = c["qks"][h4 // 2]
                b3 = (h4 % 2) * 3
                return qk2[:, b3 + 0, :], qk2[:, b3 + 1, :], qk2[:, b3 + 2, :], qk2res

            def mid(n, hh=hh):
                c = ctx[n]
                curs = [sb_par[hh * 4 + h4] for h4 in H4]
                if full:
                    c["ost"], c["ostres"] = newstat()
                    c["osb"], c["osres"] = o_sb[0], "o_sb0"
                    ost, ostres, osb, osres = c["ost"], c["ostres"], c["osb"], c["osres"]
                    for h4 in H4:
                        hd = hh * 4 + h4
                        qT_, qtT_, kT_, qkres = qk_of(c, h4)
                        pi = (n % 2) * 4 + h4
                        P.pe(I_mm(SB_[:, hsl[h4]], pT[pi], vv[:, n, hsl[h4]], True, False),
                             reads=["pT%d" % (n % 2), ("vv", n)], writes=["psf4"])
                        P.pe(I_mm(SB_[:, hsl[h4]], qtT_, stateb[curs[h4]][:, hd, :], False, True),
                             reads=[qkres, ("stateb%d" % curs[h4], hd)], writes=["psf4"])
                    for h4 in H4:
                        P.act(I_act(osb[:, h4, :], SB_[:, hsl[h4]], AF.Copy, accum_out=ost[:, h4:h4 + 1]),
                              reads=["psf4"], writes=[osres, ostres])
                        jk, jkres = junkA()
                        P.act(I_act(jk[:, 0:128], SB_[:, hsl[h4]], AF.Square,
                                    accum_out=ost[:, 4 + h4:5 + h4]), reads=["psf4"],
                              writes=[ostres, jkres])
                for h4 in H4:
                    P.pe(I_mm(SC[:, hsl[h4]], c["ktb"][:, h4, :], vv[:, n, hsl[h4]], True, True),
                         reads=[c["ktres"], ("vv", n)], writes=["psf5"])
                for h4 in H4:
                    hd = hh * 4 + h4
                    P.dve(I_stt(statef[:, hd, :], statef[:, hd, :], float(BD[hd]), SC[:, hsl[h4]],
                                ALU.mult, ALU.add), reads=[("statef", hd), "psf5"],
                          writes=[("statef", hd)])
                for h4 in H4:
                    hd = hh * 4 + h4
                    cur = curs[h4]
                    P.act(I_acopy(stateb[1 - cur][:, hd, :], statef[:, hd, :]),
                          reads=[("statef", hd)], writes=[("stateb%d" % (1 - cur), hd)])
                    sb_par[hd] = 1 - cur

            def tail(n, hh=hh):
                if not full:
                    return
                c = ctx[n]
                ost, ostres, osb, osres = c["ost"], c["ostres"], c["osb"], c["osres"]
                s2, s2res = newstat()
                P.dve(I_ts(ost[:, 8:12], ost[:, 0:4], 1.0 / 128, ALU.mult), reads=[ostres],
                      writes=[ostres])
                P.dve(I_tt(ost[:, 12:16], ost[:, 8:12], ost[:, 8:12], ALU.mult), reads=[ostres],
                      writes=[ostres])
                P.dve(I_stt(s2[:, 0:4], ost[:, 4:8], 1.0 / 128, ost[:, 12:16], ALU.mult,
                            ALU.subtract), reads=[ostres], writes=[s2res])
                P.act(I_act(s2[:, 4:8], s2[:, 0:4], AF.Sqrt, bias=GN_EPS), reads=[s2res],
                      writes=[s2res])
                P.dve(I_recip(s2[:, 8:12], s2[:, 4:8]), reads=[s2res], writes=[s2res])
                for h4 in range(4):
                    P.dve(I_ts(yn[:, h4, :], osb[:, h4, :], ost[:, 8 + h4:9 + h4], ALU.subtract,
                               s2[:, 8 + h4:9 + h4], ALU.mult), reads=[osres, ostres, s2res],
                          writes=["yn"])
                ynf = yn.rearrange("p a b -> p (a b)")
                P.dve(I_tt(ynf, ynf, gret[:, hh * 512:(hh + 1) * 512], ALU.mult),
                      reads=["yn", "gret"], writes=["yn"])
                rb, rbres = retb[n % 2], "retb%d" % (n % 2)
                P.dve(I_tt(rb, ynf, sg[:, n, :], ALU.mult), reads=["yn", ("sg", n)], writes=[rbres])
                tb, tbres = tbank()
                for h4 in range(4):
                    P.pe(I_tr(tb[:, h4 * 128:(h4 + 1) * 128], rb[:, h4 * 128:(h4 + 1) * 128], ident),
                         reads=[rbres, "ident"], writes=[tbres])
                P.act(I_acopy(featT[:, hh * 4:hh * 4 + 4, n * 128:(n + 1) * 128],
                              tb[:, 0:512].rearrange("p (a b) -> p a b", a=4)),
                      reads=[tbres], writes=[("featT", n)])

            front(0)
            for n in range(4):
                mid(n)
                if n + 1 < 4:
                    front(n + 1)
                tail(n)
        if full and st == 2:
            dump("featT_ret", featT[:, 0:8, :], "featT", [128, 8, 512], BF16)
            dump("statef", statef, "statef", [128, 8, 128])

    def phase_lru(st, full):
        P.fence()
        A.top = mark_dense
        lrup = A.alloc([8, 9])
        wg = A.alloc([2048], BF16)
        P.dma("sp", I_dma(lrup, lrupd), writes=["lrup"])
        P.dma("pool", I_dma(wg, wgd), writes=["wg"])
        wg4 = wg.rearrange("p (t g j) -> p t g j", t=2, g=8)
        cl = A.alloc([8])
        cl2 = A.alloc([8])
        e1 = A.alloc([8])
        l1 = A.alloc([8])
        P.act(I_act(e1, lrup[:, :, 7], AF.Exp, scale=-1.0), reads=["lrup"], writes=["e1"])
        P.act(I_act(l1, e1, AF.Ln, bias=1.0), reads=["e1"], writes=["l1"])
        P.dve(I_ts(cl, l1, -8.0, ALU.mult), reads=["l1"], writes=["cl"])
        P.dve(I_ts(cl2, l1, -16.0, ALU.mult), reads=["l1"], writes=["cl"])
        xbt = [A.alloc([520]), A.alloc([520])]
        xc = [A.alloc([512]), A.alloc([512])]
        xcb = [A.alloc([512], BF16), A.alloc([512], BF16)]
        r_ = [A.alloc([512]), A.alloc([512])]
        i_ = [A.alloc([512]), A.alloc([512])]
        a_ = [A.alloc([512]), A.alloc([512])]
        s_ = [A.alloc([512]), A.alloc([512])]
        hseq = [A.alloc([512]) for _ in range(4)]
        gy = [A.alloc([512]), A.alloc([512])]
        y_ = [A.alloc([512]), A.alloc([512])]
        ysq = [A.alloc([512], BF16), A.alloc([512], BF16)]
        ssrow = y_[0]
        order = [("xb", 0), ("yb", 0), ("xb", 1), ("yb", 1)] if full else [("xb", 0), ("xb", 1)]
        groups = [(w_in, 4096 + (0 if k == "xb" else 1024) + cg * 512, k, cg) for k, cg in order]
        lb_ctr = [0]

        def lbank():
            i = lb_ctr[0] % 5
            lb_ctr[0] += 1
            return psf[i], "psf%d" % i

        def body(gi, wb, wbres):
            k, cg = groups[gi][2], groups[gi][3]
            for pair in range(2):
                cs4 = (pair * 2, pair * 2 + 1)
                banks = {}
                for c4 in cs4:
                    conv_issue(2)
                    bank, bres = lbank()
                    banks[c4] = (bank, bres)
                    for kc in range(16):
                        P.pe(I_mm(bank[:, :], wb[:, kc, c4 * 128:(c4 + 1) * 128], actT[:, kc, :],
                                  kc == 0, kc == 15), reads=["actT", wbres], writes=[bres])
                if k == "xb":
                    for c4 in cs4:
                        c, j = cg * 4 + c4, c4 % 2
                        bank, bres = banks[c4]
                        xb, xbres = xbt[j], "xbt%d" % j
                        P.dve(I_copy(xb[:, 0:3], xhist[:, c, :]), reads=[("xhist", c)], writes=[xbres])
                        P.act(I_acopy(xb[:, 3:515], bank[:, :]), reads=[bres], writes=[xbres])
                        P.dve(I_copy(xhist[:, c, :], xb[:, 512:515]), reads=[xbres], writes=[("xhist", c)])
                    for c4 in cs4:
                        c, j = cg * 4 + c4, c4 % 2
                        xb, xbres = xbt[j], "xbt%d" % j
                        x_c, xcres = xc[j], "xc%d" % j
                        P.dve(I_ts(x_c, xb[:, 0:512], lrup[:, c, 0:1], ALU.mult, lrup[:, c, 4:5], ALU.add),
                              reads=[xbres, "lrup"], writes=[xcres])
                        for tap in range(1, 4):
                            P.dve(I_stt(x_c, xb[:, tap:tap + 512], lrup[:, c, tap:tap + 1], x_c,
                                        ALU.mult, ALU.add), reads=[xbres, "lrup", xcres], writes=[xcres])
                        P.act(I_acopy(xcb[j], x_c), reads=[xcres], writes=["xcb%d" % j])
                    gb_ = {}
                    for c4 in cs4:
                        c, j = cg * 4 + c4, c4 % 2
                        br, brres = lbank()
                        bi, bires = lbank()
                        gb_[c4] = (br, brres, bi, bires)
                        P.pe(I_mm(br[:, :], wg4[:, 0, c, :], xcb[j], True, True), reads=["wg", "xcb%d" % j],
                             writes=[brres])
                        P.pe(I_mm(bi[:, :], wg4[:, 1, c, :], xcb[j], True, True), reads=["wg", "xcb%d" % j],
                             writes=[bires])
                    for c4 in cs4:
                        c, j = cg * 4 + c4, c4 % 2
                        br, brres, bi, bires = gb_[c4]
                        P.act(I_act(r_[j], br[:, :], AF.Sigmoid, bias=lrup[:, c, 5:6]), reads=[brres, "lrup"],
                              writes=["r_%d" % j])
                        P.act(I_act(i_[j], bi[:, :], AF.Sigmoid, bias=lrup[:, c, 6:7]), reads=[bires, "lrup"],
                              writes=["i_%d" % j])
                    for c4 in cs4:
                        c, j = cg * 4 + c4, c4 % 2
                        P.act(I_act(a_[j], r_[j], AF.Exp, scale=cl[:, c:c + 1]), reads=["r_%d" % j, "cl"],
                              writes=["a_%d" % j])
                        P.act(I_act(s_[j], r_[j], AF.Exp, scale=cl2[:, c:c + 1]), reads=["r_%d" % j, "cl"],
                              writes=["s_%d" % j])
                    for c4 in cs4:
                        j = c4 % 2
                        P.act(I_act(s_[j], s_[j], AF.Sqrt, scale=-1.0, bias=1.0), reads=["s_%d" % j],
                              writes=["s_%d" % j])
                    for c4 in cs4:
                        c, j = cg * 4 + c4, c4 % 2
                        P.dve(I_tt(i_[j], i_[j], xc[j], ALU.mult), reads=["i_%d" % j, "xc%d" % j],
                              writes=["i_%d" % j])
                        P.dve(I_tt(i_[j], i_[j], s_[j], ALU.mult), reads=["i_%d" % j, "s_%d" % j],
                              writes=["i_%d" % j])
                        hq, hres = hseq[c4], "hseq%d" % c4
                        P.dve(lambda e, hq=hq, c=c, j=j: e.tensor_tensor_scan(
                            out=hq, data0=a_[j], data1=i_[j], initial=lstate[:, c:c + 1],
                            op0=ALU.mult, op1=ALU.add), reads=["a_%d" % j, "i_%d" % j, ("lstate", c)],
                            writes=[hres])
                        P.act(I_acopy(lstate[:, c:c + 1], hq[:, 511:512]), reads=[hres],
                              writes=[("lstate", c)])
                else:
                    for c4 in cs4:
                        j = c4 % 2
                        bank, bres = banks[c4]
                        P.act(I_act(gy[j], bank[:, :], AF.Gelu_apprx_tanh), reads=[bres], writes=["gy%d" % j])
                    for c4 in cs4:
                        j = c4 % 2
                        hq, hres = hseq[c4], "hseq%d" % c4
                        P.dve(I_tt(y_[j], hq, gy[j], ALU.mult), reads=[hres, "gy%d" % j], writes=["y%d" % j])
                    for c4 in cs4:
                        c, j = cg * 4 + c4, c4 % 2
                        P.act(I_act(ysq[j], y_[j], AF.Square), reads=["y%d" % j], writes=["ysq%d" % j])
                        P.act(I_act(featT[:, 8 + c, :], y_[j], AF.Copy, scale=lrup[:, c, 8:9]),
                              reads=["y%d" % j, "lrup"], writes=[("featT", "l%d" % c)])
                    for c4 in cs4:
                        c, j = cg * 4 + c4, c4 % 2
                        P.pe(I_mm(SC[0:1, :], ones_bf[:, 0:1], ysq[j], c == 0, c == 7),
                             reads=["ones_bf", "ysq%d" % j], writes=["psf5"])

        stream(groups, body, (w_out, 0) if full else None)
        if full:
            P.act(I_acopy(ssrow[0:1, :], SC[0:1, :]), reads=["psf5"], writes=["y0"])
            for n in range(4):
                P.pe(I_mm(SA[:, n:n + 1], ssrow[0:1, n * 128:(n + 1) * 128], one_f[0:1, 0:1], True, True),
                     reads=["y0", "one_f"], writes=["psf3"])
            s, sres = newstat()
            P.act(I_act(s[:, 0:4], SA[:, 0:4], AF.Sqrt, scale=1.0 / 1024, bias=NORM_EPS),
                  reads=["psf3"], writes=[sres])
            P.dve(I_recip(rstd_lru[:, 0:4], s[:, 0:4]), reads=[sres], writes=["rstd_lru"])
            if st == 2:
                dump("featT_lru", featT[:, 8:16, :], "featT", [128, 8, 512], BF16)
                dump("rstd_lru", rstd_lru, "rstd_lru", [128, 8])

    def apply_flag():
        P.fence()
        sf = statef.rearrange("p a b -> p (a b)")
        P.dve(I_ts(sf, sf, flag[:, 0:1], ALU.mult), reads=["statef", "flag"], writes=["statef"])
        for hd in range(8):
            P.act(I_acopy(stateb[sb_par[hd]][:, hd, :], statef[:, hd, :]), reads=["statef"],
                  writes=[("stateb%d" % sb_par[hd], hd)])
        P.dve(I_ts(lstate, lstate, flag[:, 0:1], ALU.mult), reads=["lstate", "flag"], writes=["lstate"])
        xh = xhist.rearrange("p a b -> p (a b)")
        P.dve(I_ts(xh, xh, flag[:, 0:1], ALU.mult), reads=["xhist", "flag"], writes=["xhist"])

    def phase_wout(st):
        P.fence()
        A.top = mark_dense
        for n in range(4):
            r0 = st * 512 + n * 128
            P.dma("sp", I_dma(h[:, n, :], xw[r0:r0 + 128, :]), writes=[("h", n)])
        groups = [(w_out, cg * 512) for cg in range(4)]

        def body(cg, wb, wbres):
            cs_ = slice(cg * 512, (cg + 1) * 512)
            for n in range(4):
                ba, bares = mmbank()
                for kc in range(8):
                    P.pe(I_mm(ba[:, :], featT[:, kc, n * 128:(n + 1) * 128], wb[:, kc, :], kc == 0, kc == 7),
                         reads=["featT", wbres], writes=[bares])
                bb, bbres = mmbank()
                for kc in range(8, 16):
                    P.pe(I_mm(bb[:, :], featT[:, kc, n * 128:(n + 1) * 128], wb[:, kc, :], kc == 8, kc == 15),
                         reads=["featT", wbres], writes=[bbres])
                P.dve(I_tt(h[:, n, cs_], ba[:, :], h[:, n, cs_], ALU.add), reads=[bares, ("h", n)],
                      writes=[("h", n)])
                P.dve(I_stt(h[:, n, cs_], bb[:, :], rstd_lru[:, n:n + 1], h[:, n, cs_], ALU.mult, ALU.add),
                      reads=[bbres, "rstd_lru", ("h", n)], writes=[("h", n)])

        stream(groups, body, (w_xk, 0) if st == 2 else (w_xq, 0))
        if st == 2:
            dump("h1", h, "h", [128, 4, D])

    def phase_xattn(st):
        P.fence()
        A.top = mark_dense
        if st == 2:
            xs1 = A.alloc([D])
            xs = [xs1, xs1]
            mnT = A.alloc([16, 256], BF16)
        qTh = [A.alloc([4, 512], BF16), A.alloc([4, 512], BF16)]
        pexp = [A.alloc([256]) for _ in range(4)]
        pn = [A.alloc([256], BF16) for _ in range(4)]
        PT = [A.alloc([2, 512], BF16), A.alloc([2, 512], BF16)]
        if st == 2:
          load_gain(g_mem)
          for mb in range(2):
            P.dma("sp", I_dma(xs[mb], memd[mb * 128:(mb + 1) * 128, :]), writes=["xsm"])
            norm_block(xs[mb], "xsm", mb, mnT, "mnT", mb * 128)
        groups = [(w_xk, cg * 512, "k") for cg in range(4)] + [(w_xv, cg * 512, "v") for cg in range(4)]

        def body_kv(gi, wb, wbres):
            kind, cg = groups[gi][2], gi % 4
            if kind == "k":
                for fc in range(4):
                    bank, bres = mmbank()
                    for kc in range(16):
                        P.pe(I_mm(bank[:, 0:256], wb[:, kc, fc * 128:(fc + 1) * 128], mnT[:, kc, :],
                                  kc == 0, kc == 15), reads=["mnT", wbres], writes=[bres])
                    P.act(I_acopy(kTm[:, cg * 4 + fc, :], bank[:, 0:256]), reads=[bres],
                          writes=[("kTm", cg * 4 + fc)])
            else:
                for mb in range(2):
                    bank, bres = mmbank()
                    for kc in range(16):
                        P.pe(I_mm(bank[:, :], mnT[:, kc, mb * 128:(mb + 1) * 128], wb[:, kc, :],
                                  kc == 0, kc == 15), reads=["mnT", wbres], writes=[bres])
                    P.act(I_acopy(vm[:, mb, cg * 512:(cg + 1) * 512], bank[:, :]), reads=[bres],
                          writes=[("vm", cg)])

        if st == 2:
            stream(groups, body_kv, (w_xq, 0))
        load_gain(g_xat)
        norm_multi([(h[:, n, :], ("h", n), n, n * 128) for n in range(4)], actT, "actT")
        groups_q = [(w_xq, hd * 512) for hd in range(4)]
        scale = 512.0 ** -0.5

        def body_q(hd, wb, wbres):
            qT, qres = qTh[hd % 2], "qTh%d" % (hd % 2)
            for fc in range(4):
                bank, bres = mmbank()
                for kc in range(16):
                    P.pe(I_mm(bank[:, :], wb[:, kc, fc * 128:(fc + 1) * 128], actT[:, kc, :],
                              kc == 0, kc == 15), reads=["actT", wbres], writes=[bres])
                P.act(I_act(qT[:, fc, :], bank[:, :], AF.Copy, scale=scale), reads=[bres],
                      writes=[(qres, fc)])
            pt, ptres = PT[hd % 2], "PT%d" % (hd % 2)
            sbk = [(SA, "psf3"), (SB_, "psf4")]
            sts = [newstat() for _ in range(4)]
            for n in range(4):
                bk, bkres = sbk[n // 2]
                cs2 = slice((n % 2) * 256, (n % 2) * 256 + 256)
                for fc in range(4):
                    P.pe(I_mm(bk[:, cs2], qT[:, fc, n * 128:(n + 1) * 128], kTm[:, hd * 4 + fc, :],
                              fc == 0, fc == 3), reads=[qres, ("kTm", hd * 4 + fc)], writes=[bkres])
            for n in range(4):
                bk, bkres = sbk[n // 2]
                cs2 = slice((n % 2) * 256, (n % 2) * 256 + 256)
                s, sres = sts[n]
                P.dve(lambda e, s=s, bk=bk, cs2=cs2: e.reduce_max(out=s[:, 0:1], in_=bk[:, cs2], axis=AX.X),
                      reads=[bkres], writes=[sres])
            for n in range(4):
                s, sres = sts[n]
                P.dve(I_ts(s[:, 1:2], s[:, 0:1], -1.0, ALU.mult), reads=[sres], writes=[sres])
            for n in range(4):
                bk, bkres = sbk[n // 2]
                cs2 = slice((n % 2) * 256, (n % 2) * 256 + 256)
                s, sres = sts[n]
                P.act(I_act(pexp[n], bk[:, cs2], AF.Exp, bias=s[:, 1:2], accum_out=s[:, 2:3]),
                      reads=[bkres, sres], writes=["pexp%d" % n, sres])
            for n in range(4):
                s, sres = sts[n]
                P.dve(I_recip(s[:, 3:4], s[:, 2:3]), reads=[sres], writes=[sres])
            for n in range(4):
                s, sres = sts[n]
                P.dve(I_ts(pn[n], pexp[n], s[:, 3:4], ALU.mult), reads=["pexp%d" % n, sres],
                      writes=["pn%d" % n])
            for pr in range(2):
                tb, tbres = tbank()
                for nl in range(2):
                    n = 2 * pr + nl
                    for mb in range(2):
                        o2 = (nl * 2 + mb) * 128
                        P.pe(I_tr(tb[:, o2:o2 + 128], pn[n][:, mb * 128:(mb + 1) * 128], ident),
                             reads=["pn%d" % n, "ident"], writes=[tbres])
                for nl in range(2):
                    n = 2 * pr + nl
                    P.act(I_acopy(pt[:, :, n * 128:(n + 1) * 128],
                                  tb[:, nl * 256:nl * 256 + 256].rearrange("p (a b) -> p a b", a=2)),
                          reads=[tbres], writes=[(ptres, n)])
            for dc in range(4):
                bank, bres = mmbank()
                for mb in range(2):
                    P.pe(I_mm(bank[:, :], vm[:, mb, hd * 512 + dc * 128:hd * 512 + (dc + 1) * 128],
                              pt[:, mb, :], mb == 0, mb == 1), reads=[("vm", hd), ptres], writes=[bres])
                P.act(I_acopy(featT[:, hd * 4 + dc, :], bank[:, :]), reads=[bres],
                      writes=[("featT", hd * 4 + dc)])

        stream(groups_q, body_q, (w_xo, 0))
        groups_o = [(w_xo, cg * 512) for cg in range(4)]

        def body_o(cg, wb, wbres):
            cs_ = slice(cg * 512, (cg + 1) * 512)
            for n in range(4):
                bank, bres = mmbank()
                for kc in range(16):
                    P.pe(I_mm(bank[:, :], featT[:, kc, n * 128:(n + 1) * 128], wb[:, kc, :], kc == 0, kc == 15),
                         reads=["featT", wbres], writes=[bres])
                P.dve(I_tt(h[:, n, cs_], bank[:, :], h[:, n, cs_], ALU.add), reads=[bres, ("h", n)],
                      writes=[("h", n)])

        stream(groups_o, body_o, (w_pq, 0))
        if st == 2:
            dump("h2", h, "h", [128, 4, D])

    def phase_peersel(st):
        P.fence()
        A.top = mark_dense
        qTp = featT
        if os.environ.get("KDBG"):
            print("conv issued before peersel flush st", st, conv["next"])
        conv_issue(256)
        load_gain(g_ffn)
        norm_multi([(h[:, n, :], ("h", n), n, n * 128) for n in range(4)], actT, "actT")
        groups = [(w_pq, cg * 512) for cg in range(4)]

        def body(cg, wb, wbres):
            for fc in range(4):
                bank, bres = mmbank()
                for kc in range(16):
                    P.pe(I_mm(bank[:, :], wb[:, kc, fc * 128:(fc + 1) * 128], actT[:, kc, :],
                              kc == 0, kc == 15), reads=["actT", wbres], writes=[bres])
                P.act(I_acopy(qTp[:, cg * 4 + fc, :], bank[:, :]), reads=[bres], writes=[("featT", cg * 4 + fc)])

        stream(groups, body)

    def phase_gather(st):
        P.fence()
        A.top = mark_feat
        qTp = featT
        skT = A.alloc([2048], BF16)
        P.dma("pool", I_dma(skT, skTd), writes=["skT"])
        skT3 = skT.rearrange("p (a b) -> p a b", a=16)
        iok = A.alloc([128], I32)
        P.pool(lambda e: e.iota(iok, pattern=[[1, 128]], base=0, channel_multiplier=0), writes=["iok"])
        sc = A.alloc([16, 128])
        scr = A.alloc([2048])
        sc2 = scr.rearrange("p (a b) -> p a b", a=16)
        pk2 = scr.rearrange("p (a b) -> p a b", a=8)
        vals = A.alloc([16, 16])
        idl = A.alloc([16, 16], I32)
        cand = sc.rearrange("p a b -> p (a b)").rearrange("p (a b) -> p a b", a=8)
        cid = scr.bitcast(I32).rearrange("p (a b) -> p a b", a=8)
        top = A.alloc([8, 16])
        tsc = A.alloc([8, 16])
        ex = A.alloc([8, 16])
        sci = sc.bitcast(I32)
        vali = vals.bitcast(I32)
        candi = cand.bitcast(I32)
        topi = top.bitcast(I32)
        sbanks = [(SB_, "psf4"), (SC, "psf5")]

        def sel_ops(n):
            ops = []
            for q4 in range(4):
                bank, bres = sbanks[q4 % 2]

                def f(q4=q4, bank=bank, bres=bres):
                    for j in range(4):
                        p16 = q4 * 4 + j
                        P.pe(I_mm(bank[:, j * 128:(j + 1) * 128], qTp[:, p16, n * 128:(n + 1) * 128],
                                  skT3[:, p16, :], True, True), reads=[("featT", p16), "skT"], writes=[bres])
                    P.act(I_acopy(sc[:, q4 * 4:(q4 + 1) * 4, :], bank[:, :].rearrange("p (a b) -> p a b", a=4)),
                          reads=[bres], writes=["sc"])
                ops.append(f)
            ops.append(lambda: P.dve(I_tss(sci, sci, -128, ALU.bitwise_and), reads=["sc"], writes=["sc"]))
            ops.append(lambda: P.dve(I_tt(sci, sci, iok.unsqueeze(1).to_broadcast([128, 16, 128]), ALU.bitwise_or),
                                     reads=["sc", "iok"], writes=["sc"]))
            for p16 in range(16):
                ops.append(lambda p16=p16: P.dve(lambda e: e.max(out=vals[:, p16, 0:8], in_=sc[:, p16, :]),
                                                 reads=["sc"], writes=[("vals", p16)]))
                ops.append(lambda p16=p16: P.dve(lambda e: e.match_replace(
                    out=sc2[:, p16, :], in_to_replace=vals[:, p16, 0:8], in_values=sc[:, p16, :], imm_value=NEG),
                    reads=["sc", ("vals", p16)], writes=["scr"]))
                ops.append(lambda p16=p16: P.dve(lambda e: e.max(out=vals[:, p16, 8:16], in_=sc2[:, p16, :]),
                                                 reads=["scr"], writes=[("vals", p16)]))
            id4 = idl.rearrange("p (h c) k -> p h c k", c=2)
            v4 = vals.rearrange("p (h c) k -> p h c k", c=2)
            c4 = cand.rearrange("p h (a b) -> p h a b", a=16)
            ci4 = cid.rearrange("p h (a b) -> p h a b", a=16)
            ops.append(lambda: P.dve(I_tss(idl, vali, 127, ALU.bitwise_and), reads=["vals"], writes=["idl"]))
            ops.append(lambda: P.dve(I_tss(id4[:, :, 0, :], id4[:, :, 0, :], 7, ALU.logical_shift_left),
                                     reads=["idl"], writes=["idl"]))
            ops.append(lambda: P.dve(I_tt(c4, v4[:, :, 0, :].unsqueeze(3).to_broadcast([128, 8, 16, 16]),
                                          v4[:, :, 1, :].unsqueeze(2).to_broadcast([128, 8, 16, 16]), ALU.add),
                                     reads=["vals"], writes=["sc"]))
            ops.append(lambda: P.dve(I_tt(ci4, id4[:, :, 0, :].unsqueeze(3).to_broadcast([128, 8, 16, 16]),
                                          id4[:, :, 1, :].unsqueeze(2).to_broadcast([128, 8, 16, 16]),
                                          ALU.bitwise_or), reads=["idl"], writes=["scr"]))
            ops.append(lambda: P.dve(I_tss(candi, candi, -16384, ALU.bitwise_and), reads=["sc"], writes=["sc"]))
            ops.append(lambda: P.dve(I_tt(candi, candi, cid, ALU.bitwise_or), reads=["sc", "scr"], writes=["sc"]))
            for hd in range(8):
                ops.append(lambda hd=hd: P.dve(lambda e: e.max(out=top[:, hd, 0:8], in_=cand[:, hd, :]),
                                               reads=["sc"], writes=[("top", hd)]))
                ops.append(lambda hd=hd: P.dve(lambda e: e.match_replace(
                    out=pk2[:, hd, :], in_to_replace=top[:, hd, 0:8], in_values=cand[:, hd, :], imm_value=NEG),
                    reads=["sc", ("top", hd)], writes=["scr"]))
                ops.append(lambda hd=hd: P.dve(lambda e: e.max(out=top[:, hd, 8:16], in_=pk2[:, hd, :]),
                                               reads=["scr"], writes=[("top", hd)]))
            ops.append(lambda: P.dve(I_tss(eid[:, n, :], topi.rearrange("p a b -> p (a b)"), 16383, ALU.bitwise_and),
                                     reads=["top"], writes=[("eid", n)]))
            ops.append(lambda: P.dve(I_tss(tsc.bitcast(I32), topi, -16384, ALU.bitwise_and), reads=["top"],
                                     writes=["tsc"]))
            ops.append(lambda: P.dve(I_tt(tsc, tsc, top[:, :, 0:1].to_broadcast([128, 8, 16]), ALU.subtract),
                                     reads=["tsc", "top"], writes=["tsc"]))
            ops.append(lambda: P.act(I_act(ex, tsc, AF.Exp), reads=["tsc"], writes=["ex"]))

            def fin():
                s, sres = newstat()
                P.dve(lambda e: e.tensor_reduce(out=s[:, 0:8], in_=ex, axis=AX.X, op=ALU.add),
                      reads=["ex"], writes=[sres])
                P.dve(I_recip(s[:, 8:16], s[:, 0:8]), reads=[sres], writes=[sres])
                P.dve(I_tt(gate[:, n, :].rearrange("p (a b) -> p a b", a=8), ex,
                           s[:, 8:16].unsqueeze(2).to_broadcast([128, 8, 16]), ALU.mult),
                      reads=["ex", sres], writes=[("gate", n)])
            ops.append(fin)
            return ops

        NB = 8
        gb = [A.alloc([2 * D], BF16) for _ in range(NB)]
        gfin = A.alloc([D])
        NPR = 2
        prod = junk_bufs
        actv = A.alloc([128])
        ge = A.alloc([128])
        gw = A.alloc([128])
        NDG = 4
        dg = [A.alloc([128], BF16) for _ in range(NDG)]
        P.dma("sp", I_dma(gfin, g_fin), writes=["gfin"])
        gc = [0]
        for f in sel_ops(0):
            f()
        for n in range(4):
            nxt_ops = sel_ops(n + 1) if n + 1 < 4 else []
            n_sel = len(nxt_ops)
            if os.environ.get("KINT", "1") != "1":
                while nxt_ops:
                    nxt_ops.pop(0)()
            LAG = 0
            binfo = {}

            def stage_b(sl):
                i, j, gres = binfo.pop(sl)
                P.act(I_act(gw[:, sl:sl + 1], ge[:, sl:sl + 1], AF.Copy, scale=gate[:, n, sl:sl + 1]),
                      reads=[("ge", sl), ("gate", n)], writes=[("gw", sl)])
                P.act(I_act(dg[j], ident, AF.Copy, scale=gw[:, sl:sl + 1]),
                      reads=["ident", ("gw", sl)], writes=["dg%d" % j])
                for dc in range(4):
                    P.pe(I_mm(psf[dc][:, :], dg[j], gb[i][:, D + dc * 512:D + (dc + 1) * 512],
                              sl == 0, sl == 127), reads=["dg%d" % j, gres], writes=["psf%d" % dc])

            for sl in range(128):
                i = gc[0] % NB
                j = gc[0] % NDG
                jp = gc[0] % NPR
                gc[0] += 1
                gres = "gb%d" % i
                pres = "junkA%d" % jp
                binfo[sl] = (i, j, gres)
                P.dma("pool", lambda e, i=i, n=n, sl=sl: e.indirect_dma_start(
                    out=gb[i], out_offset=None, in_=uv,
                    in_offset=IndirectOffsetOnAxis(ap=eid[:, n, sl:sl + 1], axis=0)),
                    reads=[("eid", n), "uv"], writes=[gres])
                P.dve(I_stt(prod[jp], gb[i][:, 0:D], 1.0, xn[:, n, :], ALU.mult, ALU.mult,
                            accum_out=actv[:, sl:sl + 1]), reads=[gres, ("xn", n)],
                      writes=[("actv", sl), pres])
                P.act(I_act(ge[:, sl:sl + 1], actv[:, sl:sl + 1], AF.Gelu_apprx_tanh),
                      reads=[("actv", sl)], writes=[("ge", sl)])
                if sl >= LAG:
                    stage_b(sl - LAG)
                if sl % 16 == 8:
                    for _ in range(-(-n_sel // 8)):
                        if nxt_ops:
                            nxt_ops.pop(0)()
            for sl in range(128 - LAG, 128):
                stage_b(sl)
            while nxt_ops:
                nxt_ops.pop(0)()
            for dc in range(4):
                cs_ = slice(dc * 512, (dc + 1) * 512)
                P.dve(I_tt(h[:, n, cs_], psf[dc][:, :], h[:, n, cs_], ALU.add),
                      reads=["psf%d" % dc, ("h", n)], writes=[("h", n)])
            s, sres = newstat()
            jk, jkres = junkA()
            P.act(I_act(jk, h[:, n, :], AF.Square, accum_out=s[:, 0:1]), reads=[("h", n)],
                  writes=[sres, jkres])
            P.act(I_act(s[:, 1:2], s[:, 0:1], AF.Sqrt, scale=1.0 / D, bias=NORM_EPS), reads=[sres],
                  writes=[sres])
            P.dve(I_recip(s[:, 2:3], s[:, 1:2]), reads=[sres], writes=[sres])
            io = gc[0] % NB
            gc[0] += 1
            o, ores = gb[io].bitcast(F32), "gb%d" % io
            P.dve(I_stt(o, h[:, n, :], s[:, 2:3], gfin, ALU.mult, ALU.mult),
                  reads=[("h", n), sres, "gfin"], writes=[ores])
            r0 = (st - 2) * 512 + n * 128
            P.dma("sp", I_dma(outd[r0:r0 + 128, :], o), reads=[ores])
        if st == 2:
            dump("eid", eid, "eid", [128, 4, 128], I32)
            dump("gate", gate, "gate", [128, 4, 128])
            dump("h3", h, "h", [128, 4, D])

    phases = []
    for st in range(n_sub):
        full = st >= 2
        if os.environ.get("KDBG"):
            print("conv issued before st", st, conv["next"])
        conv["quota"] = 4 if st < 2 else 3
        conv["rate"] = 0.62 if st < 2 else 0.8
        phase_norm1(st)
        phase_ret(st, full)
        phase_lru(st, full)
        if st == 1:
            apply_flag()
        if full:
            if stop_after == "lru":
                break
            phase_wout(st)
            if stop_after == "wout":
                break
            phase_xattn(st)
            if stop_after == "xattn":
                break
            phase_peersel(st)
            if stop_after == "peersel":
                break
            phase_gather(st)
    stats = P.emit()
    stats["sbuf_peak_words"] = A.peak
    return nc, stats, dbg_out


_LOG_GAMMA = np.log1p(-np.exp2(-5.0 - np.arange(8, dtype=np.float64)))
BD = np.exp(128.0 * _LOG_GAMMA)


def _consts():
    lg = _LOG_GAMMA
    idx = np.arange(128, dtype=np.float64)
    i = idx[None, :]
    j = idx[:, None]
    ci, cj = (i // 64), (j // 64)
    scale = 128.0 ** -0.5
    maskT = np.zeros((128, 8, 128), np.float64)
    for hd in range(8):
        same = np.exp(np.abs(i - j) * lg[hd])
        later = np.exp((i - j) * lg[hd])
        m = np.where(ci == cj, same, np.where(ci > cj, later, 0.0))
        maskT[:, hd, :] = m * scale
    qd = np.exp((idx[:, None] + 1.0) * lg[None, :])
    kd = np.exp((127.0 - idx[:, None]) * lg[None, :]) * scale
    qkd = np.concatenate([qd, kd], axis=1)
    return maskT.astype(np.float32), qkd.astype(np.float32)


def _rope_table(pos0):
    half = 64
    inv_freq = (np.float32(10000.0) ** (-np.arange(half, dtype=np.float32) / np.float32(half))).astype(np.float32)
    pos = (pos0 + np.arange(2048)).astype(np.float32)
    ang = (pos[:, None] * inv_freq[None, :]).astype(np.float32)
    cs = np.concatenate([np.cos(ang), np.sin(ang)], axis=1).astype(np.float32)
    return np.ascontiguousarray(cs.reshape(16, 128, 128).transpose(1, 0, 2))


def _wl(w):
    k, n = w.shape
    return np.ascontiguousarray(w.reshape(k // 128, 128, n).transpose(1, 0, 2))


def _rep(v):
    return np.ascontiguousarray(np.broadcast_to(np.asarray(v, np.float32)[None, :], (128, v.shape[0])))


_CACHE = {}


def make_in_maps(inp):
    f = lambda a: np.asarray(a, dtype=np.float32)
    x = f(inp["x"])
    mem = f(inp["mem"])
    maskT, qkd = _consts()
    shared = {
        "maskT": maskT, "qkd": qkd,
        "g_mix": _rep(f(inp["mix_norm_g"])[0]), "g_xattn": _rep(f(inp["xattn_norm_g"])[0]),
        "g_mem": _rep(f(inp["mem_norm_g"])[0]), "g_ffn": _rep(f(inp["ffn_norm_g"])[0]),
        "g_final": _rep(f(inp["final_norm_g"])), "g_ret": _rep(f(inp["ret_gn_g"])[0]),
        "w_in": _wl(f(inp["w_in"])[0]), "w_out": _wl(f(inp["w_out"])[0]),
        "w_xq": _wl(f(inp["w_xq"])[0]), "w_xk": _wl(f(inp["w_xk"])[0]),
        "w_xv": _wl(f(inp["w_xv"])[0]), "w_xo": _wl(f(inp["w_xo"])[0]),
        "peer_w_q": _wl(f(inp["peer_w_q"])[0]),
        "peer_u": np.ascontiguousarray(f(inp["peer_u"])[0]),
        "peer_v": np.ascontiguousarray(f(inp["peer_v"])[0]),
    }
    cw = f(inp["conv_w"])[0]
    cols = [cw[0], cw[1], cw[2], cw[3], f(inp["conv_b"])[0], f(inp["b_rg"])[0].reshape(-1),
            f(inp["b_ig"])[0].reshape(-1), f(inp["lru_lambda"])[0], f(inp["lru_norm_g"])[0]]
    lrup = np.stack(cols, axis=-1)
    shared["lrup"] = np.ascontiguousarray(lrup.reshape(8, 128, 9).transpose(1, 0, 2))
    wrg = f(inp["w_rg"])[0]
    wig = f(inp["w_ig"])[0]
    wg = np.stack([wrg, wig], axis=0)
    shared["w_gates"] = np.ascontiguousarray(wg.transpose(2, 0, 1, 3).reshape(128, 2048))
    sk = f(inp["peer_sub_keys"])[0]
    shared["skT"] = np.ascontiguousarray(sk.transpose(3, 0, 1, 2).reshape(128, 2048))
    in_maps = []
    for c in range(8):
        b, s = c // 2, c % 2
        xwin = np.zeros((2048, D), np.float32)
        if s == 1:
            xwin[0:1024] = x[b, 0:1024]
        xwin[1024:2048] = x[b, s * 1024:(s + 1) * 1024]
        m = dict(shared)
        m["xw"] = xwin
        m["mem"] = np.ascontiguousarray(mem[b])
        m["flag"] = np.full((128, 1), float(s), np.float32)
        m["cs"] = _rope_table(s * 1024 - 1024)
        in_maps.append(m)
    return in_maps


def kernel(**inputs):
    if "nc" not in _CACHE:
        _CACHE["nc"] = build_program()[0]
    nc = _CACHE["nc"]
    in_maps = make_in_maps(inputs)
    res = run_bass_kernel_spmd(nc, in_maps, core_ids=list(range(8)))
    out = np.zeros((4, 2048, D), np.float32)
    for c in range(8):
        b, s = c // 2, c % 2
        out[b, s * 1024:(s + 1) * 1024] = np.asarray(res.results[c]["out"], np.float32)
    return out
```
